# Optimizing a Trainium2 kernel written in Bass

```python
import math
import jax, jax.numpy as jnp
from jax import lax
import numpy as np

D_MODEL = 1024
BATCH = 32
SEQ = 2048
DEPTH = 2

N_HEADS = 8
Q_LORA = 256
KV_LORA = 128
QK_NOPE = 64
QK_ROPE = 32
V_HEAD = 64
ROPE_THETA = 10000.0
Q_BLOCK = 128
D_HY = 512
SHORT_CONV = 3
FILTER_EMB = 33
FILTER_HIDDEN = 64
FAST_DECAY_PCT = 0.3
SLOW_DECAY_PCT = 1.5
DECAY_TARGET = 1e-2
MOD_SHIFT = 0.0
D_FF = -(-8 * D_MODEL // (3 * 256)) * 256
EPS = 1e-6
IN_SPLITS = (Q_LORA, Q_LORA + KV_LORA, Q_LORA + KV_LORA + QK_ROPE, Q_LORA + KV_LORA + QK_ROPE + 3 * D_HY)
IN_COLS = IN_SPLITS[-1] + 2 * D_MODEL

kernel_name = 'hybrid_mla_hyena_gated_encoder'


def rms_norm(x, g):
    xf = x.astype(jnp.float32)
    y = xf * lax.rsqrt(jnp.mean(xf * xf, axis=-1, keepdims=True) + EPS)
    return (y * g.astype(jnp.float32)).astype(x.dtype)


def rope_tables(positions):
    inv = 1.0 / (ROPE_THETA ** (jnp.arange(0, QK_ROPE, 2, dtype=jnp.float32) / QK_ROPE))
    ang = positions.astype(jnp.float32)[:, None] * inv[None, :]
    return jnp.cos(ang), jnp.sin(ang)


def apply_rope(x, cos, sin):
    xf = x.astype(jnp.float32).reshape(x.shape[:-1] + (QK_ROPE // 2, 2))
    bshape = (1, cos.shape[0]) + (1,) * (x.ndim - 3) + (cos.shape[1],)
    c = cos.reshape(bshape)
    s = sin.reshape(bshape)
    x0, x1 = xf[..., 0], xf[..., 1]
    return jnp.stack([x0 * c - x1 * s, x0 * s + x1 * c], axis=-1).reshape(x.shape).astype(x.dtype)


def mla_branch(q_lat, kv_lat, k_pe, q_norm_g, w_q_up, kv_norm_g, w_kv_up, w_attn_proj, cos, sin):
    b, s, _ = q_lat.shape
    q = (rms_norm(q_lat, q_norm_g) @ w_q_up).reshape(b, s, N_HEADS, QK_NOPE + QK_ROPE)
    q_nope = q[..., :QK_NOPE]
    q_pe = apply_rope(q[..., QK_NOPE:], cos, sin)
    kv = (rms_norm(kv_lat, kv_norm_g) @ w_kv_up).reshape(b, s, N_HEADS, QK_NOPE + V_HEAD)
    k_nope, v = kv[..., :QK_NOPE], kv[..., QK_NOPE:]
    k_pe = apply_rope(k_pe, cos, sin)
    scale = (QK_NOPE + QK_ROPE) ** -0.5
    nb = s // Q_BLOCK
    qn_blocks = jnp.moveaxis(q_nope.reshape(b, nb, Q_BLOCK, N_HEADS, QK_NOPE), 1, 0)
    qp_blocks = jnp.moveaxis(q_pe.reshape(b, nb, Q_BLOCK, N_HEADS, QK_ROPE), 1, 0)

    def attend(blk):
        qn, qp = blk
        logits = (jnp.einsum('bqhd,bkhd->bhqk', qn, k_nope, preferred_element_type=jnp.float32)
                  + jnp.einsum('bqhr,bkr->bhqk', qp, k_pe, preferred_element_type=jnp.float32)) * scale
        probs = jax.nn.softmax(logits, axis=-1).astype(v.dtype)
        return jnp.einsum('bhqk,bkhd->bqhd', probs, v)

    o = lax.map(attend, (qn_blocks, qp_blocks))
    o = jnp.moveaxis(o, 0, 1).reshape(b, s, N_HEADS * V_HEAD)
    return o @ w_attn_proj


def hyena_filters(length, w1, b1, f1, w2, b2, f2, w3, b3, f3, w4):
    f32 = jnp.float32
    t = jnp.linspace(0.0, 1.0, length, dtype=f32)[:, None]
    bands = (FILTER_EMB - 1) // 2
    freqs = jnp.linspace(1e-4, bands - 1, bands, dtype=f32)[None, :]
    w = 2.0 * math.pi * jnp.arange(length, dtype=f32)[:, None] / length
    z = jnp.concatenate([t, jnp.cos(freqs * w), -jnp.sin(freqs * w)], axis=-1)
    h = jnp.sin(f1.astype(f32) * (z @ w1.astype(f32) + b1.astype(f32)))
    h = jnp.sin(f2.astype(f32) * (h @ w2.astype(f32) + b2.astype(f32)))
    h = jnp.sin(f3.astype(f32) * (h @ w3.astype(f32) + b3.astype(f32)))
    h = h @ w4.astype(f32)
    deltas = jnp.abs(jnp.linspace(math.log(DECAY_TARGET) / FAST_DECAY_PCT,
                                  math.log(DECAY_TARGET) / SLOW_DECAY_PCT, D_HY, dtype=f32))
    decay = jnp.exp(-t * jnp.tile(deltas, 2)[None, :])
    return h * (decay + MOD_SHIFT)


def hyena_branch(u, conv_w, conv_b, w1, b1, f1, w2, b2, f2, w3, b3, f3, w4, hy_skip, w_hy_proj):
    b, length, _ = u.shape
    up = jnp.pad(u, ((0, 0), (1, 1), (0, 0)))
    uc = up[:, :-2] * conv_w[0] + up[:, 1:-1] * conv_w[1] + up[:, 2:] * conv_w[2] + conv_b
    x0, x1, v = jnp.split(uc, 3, axis=-1)
    filt = hyena_filters(length, w1, b1, f1, w2, b2, f2, w3, b3, f3, w4)
    taps = jnp.concatenate([filt[:, :D_HY], jnp.zeros((1, D_HY), jnp.float32),
                            jnp.flip(filt[1:, D_HY:], axis=0)], axis=0)
    z = (v * x1).astype(jnp.float32)
    spec = jnp.fft.rfft(z, n=2 * length, axis=1) * jnp.fft.rfft(taps, axis=0)[None]
    y = jnp.fft.irfft(spec, n=2 * length, axis=1)[:, :length] + z * hy_skip.astype(jnp.float32)
    y = (y * x0.astype(jnp.float32)).astype(u.dtype)
    return y @ w_hy_proj


def setup_inputs(seed: int = 0) -> dict:
    key = jax.random.key(seed)
    ks = iter(jax.random.split(key, 40))

    def nrm(shape, scale):
        return jax.random.normal(next(ks), shape, jnp.float32) * scale

    def gain(shape):
        return 1.0 + nrm(shape, 0.02)

    L = DEPTH
    x = nrm((BATCH, SEQ, D_MODEL), 1.0)
    positions = jnp.arange(SEQ, dtype=jnp.int32)
    mix_norm_g = gain((L, D_MODEL))
    w_in = nrm((L, D_MODEL, IN_COLS), D_MODEL ** -0.5)
    q_norm_g = gain((L, Q_LORA))
    w_q_up = nrm((L, Q_LORA, N_HEADS * (QK_NOPE + QK_ROPE)), Q_LORA ** -0.5)
    kv_norm_g = gain((L, KV_LORA))
    w_kv_up = nrm((L, KV_LORA, N_HEADS * (QK_NOPE + V_HEAD)), KV_LORA ** -0.5)
    w_attn_proj = nrm((L, N_HEADS * V_HEAD, D_MODEL), (N_HEADS * V_HEAD) ** -0.5)
    hy_conv_w = nrm((L, SHORT_CONV, 3 * D_HY), SHORT_CONV ** -0.5)
    hy_conv_b = nrm((L, 3 * D_HY), 0.02)
    filt_w1 = nrm((L, FILTER_EMB, FILTER_HIDDEN), FILTER_EMB ** -0.5)
    filt_b1 = nrm((L, FILTER_HIDDEN), 0.02)
    filt_f1 = gain((L, FILTER_HIDDEN))
    filt_w2 = nrm((L, FILTER_HIDDEN, FILTER_HIDDEN), FILTER_HIDDEN ** -0.5)
    filt_b2 = nrm((L, FILTER_HIDDEN), 0.02)
    filt_f2 = gain((L, FILTER_HIDDEN))
    filt_w3 = nrm((L, FILTER_HIDDEN, FILTER_HIDDEN), FILTER_HIDDEN ** -0.5)
    filt_b3 = nrm((L, FILTER_HIDDEN), 0.02)
    filt_f3 = gain((L, FILTER_HIDDEN))
    filt_w4 = nrm((L, FILTER_HIDDEN, 2 * D_HY), 0.1 * FILTER_HIDDEN ** -0.5)
    hy_skip = nrm((L, D_HY), 0.1)
    w_hy_proj = nrm((L, D_HY, D_MODEL), D_HY ** -0.5)
    w_out = nrm((L, D_MODEL, D_MODEL), D_MODEL ** -0.5)
    ffn_norm_g = gain((L, D_MODEL))
    w_gate = nrm((L, D_MODEL, D_FF), D_MODEL ** -0.5)
    w_up = nrm((L, D_MODEL, D_FF), D_MODEL ** -0.5)
    w_down = nrm((L, D_FF, D_MODEL), D_FF ** -0.5)
    final_norm_g = gain((D_MODEL,))
    return {'x': x, 'positions': positions, 'mix_norm_g': mix_norm_g, 'w_in': w_in,
            'q_norm_g': q_norm_g, 'w_q_up': w_q_up, 'kv_norm_g': kv_norm_g, 'w_kv_up': w_kv_up,
            'w_attn_proj': w_attn_proj, 'hy_conv_w': hy_conv_w, 'hy_conv_b': hy_conv_b,
            'filt_w1': filt_w1, 'filt_b1': filt_b1, 'filt_f1': filt_f1,
            'filt_w2': filt_w2, 'filt_b2': filt_b2, 'filt_f2': filt_f2,
            'filt_w3': filt_w3, 'filt_b3': filt_b3, 'filt_f3': filt_f3, 'filt_w4': filt_w4,
            'hy_skip': hy_skip, 'w_hy_proj': w_hy_proj, 'w_out': w_out,
            'ffn_norm_g': ffn_norm_g, 'w_gate': w_gate, 'w_up': w_up, 'w_down': w_down,
            'final_norm_g': final_norm_g}


def reference(x, positions, mix_norm_g, w_in, q_norm_g, w_q_up, kv_norm_g, w_kv_up,
              w_attn_proj, hy_conv_w, hy_conv_b, filt_w1, filt_b1, filt_f1,
              filt_w2, filt_b2, filt_f2, filt_w3, filt_b3, filt_f3, filt_w4,
              hy_skip, w_hy_proj, w_out, ffn_norm_g, w_gate, w_up, w_down, final_norm_g):
    cos, sin = rope_tables(positions)
    for i in range(DEPTH):
        h = rms_norm(x, mix_norm_g[i])
        proj = h @ w_in[i]
        q_lat, kv_lat, k_pe, hy_in, gate_logits = jnp.split(proj, IN_SPLITS, axis=-1)
        y_attn = mla_branch(q_lat, kv_lat, k_pe, q_norm_g[i], w_q_up[i], kv_norm_g[i], w_kv_up[i],
                            w_attn_proj[i], cos, sin)
        y_hy = hyena_branch(hy_in, hy_conv_w[i], hy_conv_b[i], filt_w1[i], filt_b1[i], filt_f1[i],
                            filt_w2[i], filt_b2[i], filt_f2[i], filt_w3[i], filt_b3[i], filt_f3[i],
                            filt_w4[i], hy_skip[i], w_hy_proj[i])
        gates = jax.nn.sigmoid(gate_logits.astype(jnp.float32))
        merged = (gates[..., :D_MODEL] * y_attn.astype(jnp.float32)
                  + gates[..., D_MODEL:] * y_hy.astype(jnp.float32)).astype(x.dtype)
        x = x + merged @ w_out[i]
        h = rms_norm(x, ffn_norm_g[i])
        x = x + (jax.nn.silu(h @ w_gate[i]) * (h @ w_up[i])) @ w_down[i]
    return rms_norm(x, final_norm_g)
```

```python
import math
import numpy as np
import concourse.bass as bass
import concourse.mybir as mybir
from concourse.bass_utils import run_bass_kernel_spmd
from contextlib import ExitStack

F32 = mybir.dt.float32
BF16 = mybir.dt.bfloat16
I32 = mybir.dt.int32
ALU = mybir.AluOpType
AF = mybir.ActivationFunctionType

ENGS = ['tensor', 'vector', 'scalar', 'gpsimd', 'sync']
SAME_ENGINE_SYNC = True
KB = 1024

L_SEQ = 2048
D = 1024
NH = 8
D_FF = 2816
EPS = 1e-6


class V:
    def __init__(self, ap, keys):
        self.ap = ap
        self.keys = keys

    def __getitem__(self, idx):
        return V(self.ap[idx], self.keys)

    def r(self, pat, **kw):
        return V(self.ap.rearrange(pat, **kw), self.keys)


def _keys(vs):
    out = []
    for v in vs:
        if isinstance(v, V):
            out.extend(v.keys)
        else:
            out.append(v)
    return out


class Sched:
    def __init__(self, nc, es):
        self.nc = nc
        self.es = es
        self.semh = {}
        self.cnt = {}
        for e in ENGS:
            self.semh[e] = es.enter_context(nc.semaphore('s_' + e))
            self.cnt[e] = 0
        self.prog = {e: [] for e in ENGS}
        self.seen = {e: {} for e in ENGS}
        self.st = {}
        self.nops = 0
        self.tag = ''
        self.tags = {e: [] for e in ENGS}

    def dma_sem(self, key):
        if key not in self.semh:
            self.semh[key] = self.es.enter_context(self.nc.semaphore('d_' + str(key)))
            self.cnt[key] = 0
        return key

    def _deps(self, eng, reads, writes):
        deps = {}
        st = self.st
        for b in reads:
            s = st.get(b)
            if s is not None and s[0] is not None:
                k, v = s[0]
                if deps.get(k, 0) < v:
                    deps[k] = v
        for b in writes:
            s = st.get(b)
            if s is not None:
                if s[0] is not None:
                    k, v = s[0]
                    if deps.get(k, 0) < v:
                        deps[k] = v
                for k, v in s[1].items():
                    if deps.get(k, 0) < v:
                        deps[k] = v
        seen = self.seen[eng]
        for k, v in deps.items():
            if k == eng and (eng in ('tensor', 'sync') or not SAME_ENGINE_SYNC):
                continue
            if seen.get(k, 0) < v:
                self.prog[eng].append(('wait', k, v))
                seen[k] = v

    def _record(self, ev, reads, writes):
        st = self.st
        k, v = ev
        for b in reads:
            s = st.get(b)
            if s is None:
                st[b] = [None, {k: v}]
            else:
                if s[1].get(k, 0) < v:
                    s[1][k] = v
        for b in writes:
            st[b] = [ev, {}]

    def op(self, eng, fn, reads=(), writes=(), inc=True):
        reads = _keys(reads)
        writes = _keys(writes)
        self._deps(eng, reads, writes)
        if eng != 'tensor':
            inc = True
        if inc:
            self.cnt[eng] += 1
            ev = (eng, self.cnt[eng])
        else:
            ev = (eng, self.cnt[eng] + 1)
        self._record(ev, reads, writes)
        self.prog[eng].append(('op', fn, inc))
        self.tags[eng].append(self.tag)
        self.nops += 1
        return ev

    def dma(self, queue, fns, sem, reads=(), writes=()):
        reads = _keys(reads)
        writes = _keys(writes)
        self.dma_sem(sem)
        self._deps(queue, reads, writes)
        if not isinstance(fns, (list, tuple)):
            fns = [fns]
        for fn in fns:
            self.cnt[sem] += 16
            self.prog[queue].append(('dma', fn, sem))
        ev = (sem, self.cnt[sem])
        self._record(ev, reads, writes)
        return ev

    def wait_all(self, eng, keys):
        self._deps(eng, _keys(keys), ())

    def finalize(self):
        nc = self.nc
        with nc.Block() as block:
            def mk(e):
                def body(eng):
                    for item in self.prog[e]:
                        if item[0] == 'wait':
                            eng.wait_ge(self.semh[item[1]], item[2])
                        elif item[0] == 'op':
                            r = item[1](eng)
                            if item[2]:
                                r.then_inc(self.semh[e], 1)
                        else:
                            item[1](eng).then_inc(self.semh[item[2]], 16)
                return body
            block.tensor(mk('tensor'))
            block.vector(mk('vector'))
            block.scalar(mk('scalar'))
            block.gpsimd(mk('gpsimd'))
            block.sync(mk('sync'))


class Mem:
    def __init__(self, nc, es, nbytes):
        self.t = es.enter_context(nc.sbuf_tensor("arena", [128, nbytes // 4], F32))
        self.nbytes = nbytes

    def view(self, off, n, dt, p0=0, p1=128):
        sz = 2 if dt == BF16 else 4
        nb = n * sz
        assert off % 4 == 0 and nb % 4 == 0 and off + nb <= self.nbytes, (off, nb)
        ap = self.t[p0:p1, off // 4:(off + nb) // 4]
        if dt != F32:
            ap = ap.bitcast(dt)
        keys = [(b, q) for b in range(off // KB, (off + nb - 1) // KB + 1)
                for q in range(p0 // 32, (p1 - 1) // 32 + 1)]
        return V(ap, keys)


FF_GROUPS = [(2560, 256), (0, 512), (512, 512), (1024, 512), (1536, 512), (2048, 512)]
_CONST_CACHE = {}


def _host_consts():
    if _CONST_CACHE:
        return _CONST_CACHE
    L = L_SEQ
    n = np.arange(1024, dtype=np.float64)
    CE = np.cos(2.0 * np.pi * np.outer(n, n) / 2048.0)
    SE = np.sin(2.0 * np.pi * np.outer(n, n) / 2048.0)
    CO = np.cos(2.0 * np.pi * np.outer(2 * n + 1, n) / 4096.0)
    SO = np.sin(2.0 * np.pi * np.outer(2 * n + 1, n) / 4096.0)

    def lay(T, w):
        g = 1024 // w
        return T.reshape(8, 128, g, w).transpose(2, 1, 0, 3)
    c = {}
    c['tfwd'] = np.ascontiguousarray(np.stack([lay(T, 128) for T in (CE, SE, CO, SO)], axis=2)).astype(np.float32).reshape(8, 128, 4096)
    c['tinv'] = np.ascontiguousarray(np.stack([lay(T, 256) for T in (CE, SE, CO.T, SO.T)], axis=2)).astype(np.float32).reshape(4, 128, 8192)
    f32 = np.float32
    t = np.linspace(0.0, 1.0, L, dtype=f32)[:, None]
    bands = 16
    freqs = np.linspace(1e-4, bands - 1, bands, dtype=f32)[None, :]
    w = (2.0 * math.pi * np.arange(L, dtype=f32)[:, None] / L).astype(f32)
    z = np.concatenate([t, np.cos(freqs * w), -np.sin(freqs * w)], axis=-1).astype(f32)
    c['zembT'] = np.ascontiguousarray(z.T)
    deltas = np.abs(np.linspace(math.log(1e-2) / 0.3, math.log(1e-2) / 1.5, 512, dtype=f32))
    c['decay'] = np.exp(-t * np.tile(deltas, 2)[None, :]).astype(f32)
    cst = np.zeros((128, 8), f32)
    inv = (1.0 / (10000.0 ** (np.arange(0, 32, 2, dtype=f32) / 32.0))).astype(f32)
    cst[64:80, 0] = inv / (2 * np.pi)
    cst[80:96, 0] = inv / (2 * np.pi)
    cst[64:80, 1] = -1.0
    cst[80:96, 1] = 1.0
    cst[:, 2] = 2.0 / 4096
    cst[0, 2] = 1.0 / 4096
    cst[:, 3] = 2.0 / 4096
    cst[:, 4] = EPS
    c['cst'] = cst
    cb = np.zeros((128, 258), f32)
    cb[:, 0:128] = np.eye(128)
    cb[:, 128:256] = 1.0
    cb[:, 256] = (-1.0) ** np.arange(128)
    c['cstb'] = cb
    c['altrow'] = ((-1.0) ** np.arange(256)).astype(f32)[None, :]
    _CONST_CACHE.update(c)
    return c


def _prep_weights(inp):
    f = lambda a: np.ascontiguousarray(np.asarray(a, dtype=np.float32))
    w = {}
    w_in = f(inp['w_in'])
    w['w_in'] = w_in
    ev = np.arange(0, 32, 2)
    od = ev + 1
    kpe = w_in[:, :, 384:416]
    kA = np.zeros((2, 1024, 96), np.float32)
    kB = np.zeros((2, 1024, 96), np.float32)
    kA[:, :, 64:80] = kpe[:, :, ev]
    kA[:, :, 80:96] = kpe[:, :, od]
    kB[:, :, 64:80] = kpe[:, :, od]
    kB[:, :, 80:96] = kpe[:, :, ev]
    w['wkpe'] = np.ascontiguousarray(np.stack([kA, kB], axis=2))
    wq = f(inp['w_q_up']).reshape(2, 256, 8, 96)
    qA = np.concatenate([wq[..., :64], wq[..., 64:][..., ev], wq[..., 64:][..., od]], axis=-1)
    qB = np.concatenate([wq[..., :64], wq[..., 64:][..., od], wq[..., 64:][..., ev]], axis=-1)
    w['wq'] = np.ascontiguousarray(np.stack([qA.reshape(2, 256, 768), qB.reshape(2, 256, 768)], axis=2))
    wkv = f(inp['w_kv_up']).reshape(2, 128, 8, 128)
    w['wk'] = np.ascontiguousarray(wkv[..., :64].reshape(2, 128, 512))
    w['wv'] = np.ascontiguousarray(wkv[..., 64:].reshape(2, 128, 512))
    wap = f(inp['w_attn_proj'])
    whp = f(inp['w_hy_proj'])
    mrg = np.zeros((2, 8, 128, 3072), np.float32)
    for l in range(2):
        for fc in range(8):
            ga = w_in[l][:, 1952 + fc * 128:1952 + (fc + 1) * 128].reshape(8, 128, 128).transpose(1, 0, 2).reshape(128, 1024)
            gb = w_in[l][:, 2976 + fc * 128:2976 + (fc + 1) * 128].reshape(8, 128, 128).transpose(1, 0, 2).reshape(128, 1024)
            ap = wap[l][:, fc * 128:(fc + 1) * 128].reshape(4, 128, 128).transpose(1, 0, 2).reshape(128, 512)
            hp = whp[l][:, fc * 128:(fc + 1) * 128].reshape(4, 128, 128).transpose(1, 0, 2).reshape(128, 512)
            mrg[l, fc] = np.concatenate([ga, gb, ap, hp], axis=1)
    w['mrg'] = mrg
    w['w_out'] = f(inp['w_out'])
    wg = f(inp['w_gate'])
    wu = f(inp['w_up'])
    wd = f(inp['w_down'])
    ffn = np.zeros((2, 6, 128, 12288), np.float32)
    for l in range(2):
        for gi, (g0, gs) in enumerate(FF_GROUPS):
            a = wg[l][:, g0:g0 + gs].reshape(8, 128, gs).transpose(1, 0, 2).reshape(128, 8 * gs)
            b = wu[l][:, g0:g0 + gs].reshape(8, 128, gs).transpose(1, 0, 2).reshape(128, 8 * gs)
            d = wd[l][g0:g0 + gs, :].reshape(gs // 128, 128, 1024).transpose(1, 0, 2).reshape(128, (gs // 128) * 1024)
            ffn[l, gi, :, 0:8 * gs] = a
            ffn[l, gi, :, 8 * gs:16 * gs] = b
            ffn[l, gi, :, 16 * gs:16 * gs + (gs // 128) * 1024] = d
    w['ffn'] = ffn
    w['norm_g'] = np.ascontiguousarray(np.concatenate(
        [f(inp['mix_norm_g']), f(inp['ffn_norm_g']), f(inp['final_norm_g'])[None, :]], axis=0))
    cols = np.zeros((2, 128, 64), np.float32)
    cw = f(inp['hy_conv_w'])
    cbias = f(inp['hy_conv_b'])
    for l in range(2):
        cols[l, :, 0:48] = np.stack([cw[l, 0], cw[l, 1], cw[l, 2], cbias[l]], axis=-1).reshape(12, 128, 4).transpose(1, 0, 2).reshape(128, 48)
        cols[l, :, 48:50] = f(inp['q_norm_g'])[l].reshape(2, 128).T
        cols[l, :, 50] = f(inp['kv_norm_g'])[l]
        for j, nm in enumerate(['filt_b1', 'filt_f1', 'filt_b2', 'filt_f2', 'filt_b3', 'filt_f3']):
            cols[l, 0:64, 52 + j] = f(inp[nm])[l]
    w['cols'] = cols
    w['hy_skip'] = f(inp['hy_skip'])
    w['filt_w1'] = f(inp['filt_w1'])
    w['filt_w2'] = f(inp['filt_w2'])
    w['filt_w3'] = f(inp['filt_w3'])
    w['filt_w4'] = f(inp['filt_w4'])
    w['positions'] = np.ascontiguousarray(np.asarray(inp['positions'], dtype=np.int32)).reshape(1, 2048)
    return w


X0, HT0, R10, ZY0, OT0, TB0, CS0, ARENA = 0, 65536, 98304, 131072, 147456, 163840, 196608, 210944


def build(nseq=4, nlayers=2, dbg=()):
    nc = bass.Bass("TRN2", target_bir_lowering=False)
    es = ExitStack()

    def din(name, shape, dt=F32):
        return nc.dram_tensor(name, list(shape), dt, kind="ExternalInput").ap()

    x_d = din("x", [nseq, 2048, 1024])
    pos_d = din("positions", [1, 2048], I32)
    w_in_d = din("w_in", [2, 1024, 4000])
    wkpe_d = din("wkpe", [2, 1024, 2, 96])
    wq_d = din("wq", [2, 256, 2, 768])
    wk_d = din("wk", [2, 128, 512])
    wv_d = din("wv", [2, 128, 512])
    mrg_d = din("mrg", [2, 8, 128, 3072])
    wout_d = din("w_out", [2, 1024, 1024])
    ffn_d = din("ffn", [2, 6, 128, 12288])
    ng_d = din("norm_g", [5, 1024])
    cols_d = din("cols", [2, 128, 64])
    skip_d = din("hy_skip", [2, 512])
    fw1_d = din("filt_w1", [2, 33, 64])
    fw2_d = din("filt_w2", [2, 64, 64])
    fw3_d = din("filt_w3", [2, 64, 64])
    fw4_d = din("filt_w4", [2, 64, 1024])
    tfwd_d = din("tfwd", [8, 128, 4 * 8 * 128])
    tinv_d = din("tinv", [4, 128, 4 * 8 * 256])
    zemb_d = din("zembT", [33, 2048])
    decay_d = din("decay", [2048, 1024])
    cst_d = din("cst", [128, 8])
    cstb_d = din("cstb", [128, 258])
    altrow_d = din("altrow", [1, 256])
    out_d = nc.dram_tensor("out", [nseq, 2048, 1024], F32, kind="ExternalOutput").ap()
    pspec_d = nc.dram_tensor("pspec", [2, 8, 128, 4 * 512], BF16).ap()
    mrgb_d = nc.dram_tensor("mrgb", [2, 8, 128, 3072], BF16).ap()
    dbg_out = {}

    S = Sched(nc, es)
    mem = Mem(nc, es, ARENA)
    pt = es.enter_context(nc.psum_tensor("psum", [128, 8, 512], F32))

    def PS(b, n=1):
        return V(pt[:, b:b + n, :] if n > 1 else pt[:, b, :], [('ps', b + i) for i in range(n)])

    def PSB(b):
        return V(pt[:, b, :].bitcast(BF16), [('ps', b)])

    class RR:
        def __init__(self, items):
            self.items = items
            self.i = 0

        def __call__(self):
            r = self.items[self.i % len(self.items)]
            self.i += 1
            return r

    def mm(out, lhsT, rhs, start, stop, inc=None):
        if inc is None:
            inc = stop
        S.op('tensor', lambda e: e.matmul(out.ap, lhsT=lhsT.ap, rhs=rhs.ap, start=start, stop=stop),
             reads=[lhsT, rhs], writes=[out], inc=inc)

    def tr(out, in_, ident):
        S.op('tensor', lambda e: e.transpose(out.ap, in_.ap, ident.ap), reads=[in_, ident], writes=[out])

    def vop(fn, reads, writes):
        S.op('vector', fn, reads=reads, writes=writes)

    def aop(fn, reads, writes):
        S.op('scalar', fn, reads=reads, writes=writes)

    def acopy(out, in_):
        aop(lambda e: e.activation(out=out.ap, in_=in_.ap, func=AF.Copy), [in_], [out])

    def vcopy(out, in_):
        vop(lambda e: e.tensor_copy(out=out.ap, in_=in_.ap), [in_], [out])

    cp_rr = RR(['v', 'a'])

    def anycopy(out, in_):
        if cp_rr() == 'v':
            vcopy(out, in_)
        else:
            acopy(out, in_)

    def vtt(out, a, b, op):
        vop(lambda e: e.tensor_tensor(out=out.ap, in0=a.ap, in1=b.ap, op=op), [a, b], [out])

    def wdma(out, src_ap, sem, queue='gpsimd'):
        S.dma(queue, lambda e: e.dma_start(out=out.ap, in_=src_ap), sem, writes=[out])

    def dump(name, v, shape, dt=F32):
        if name not in dbg or name in dbg_out:
            return
        d = nc.dram_tensor("dbg_" + name, list(shape), dt, kind="ExternalOutput").ap()
        dbg_out[name] = d
        S.dma('sync', lambda e: e.dma_start(out=d, in_=v.ap), 'dbg', reads=[v], writes=['DBGOUT'])

    cb = mem.view(CS0, 258, BF16)
    ident = cb[:, 0:128]
    ones = cb[:, 128:256]
    altc = cb[:, 256:257]
    cst = mem.view(CS0 + 516, 8, F32)
    epsc = cst[:, 4:5]
    colsv = [mem.view(CS0 + 548 + 256 * l, 64, F32) for l in range(2)]
    gB = mem.view(CS0 + 1060, 1024, F32)
    ROPE0 = CS0 + 5156
    cs_t = mem.view(ROPE0, 2048, BF16, 64, 96)
    sn_t = mem.view(ROPE0 + 4096, 2048, BF16, 64, 96)
    P1k = [[mem.view(ROPE0 + 2048 * l + 1024 * i, 512, BF16, 0, 1) for i in range(2)] for l in range(2)]
    G1k = [mem.view(ROPE0 + 4096 + 1024 * i, 512, BF16, 0, 1) for i in range(2)]
    altrow = mem.view(ROPE0 + 6144, 256, BF16, 0, 1)

    wdma(cb, cstb_d, 'c0')
    wdma(cst, cst_d, 'c1', 'sync')
    for l in range(2):
        wdma(colsv[l], cols_d[l], ('c2', l), 'sync')
    wdma(altrow, altrow_d, 'c3')

    Xt = [mem.view(X0 + t * 4096, 1024, F32) for t in range(16)]
    HTall = mem.view(HT0, 8 * 2048, BF16).r("p (k t) -> p k t", k=8)

    def HTs(k, t0, n):
        return mem.view(HT0 + k * 4096 + t0 * 2, n, BF16)

    def HT_tile(t):
        keys = []
        for k in range(8):
            keys += mem.view(HT0 + k * 4096 + t * 256, 128, BF16).keys
        return V(HTall.ap[:, :, t * 128:(t + 1) * 128], keys)

    def rope_tables():
        T0 = R10
        posi = mem.view(T0, 2048, I32, 64, 96)
        a = mem.view(T0 + 8192, 2048, F32, 64, 96)
        b = mem.view(T0 + 16384, 2048, F32, 64, 96)
        ki = mem.view(T0 + 24576, 2048, I32, 64, 96)
        S.dma('sync', lambda e: e.dma_start(out=posi.ap, in_=pos_d.partition_broadcast(32)), 'c4', writes=[posi])
        vcopy(a, posi)
        invc = cst[64:96, 0:1]
        sgnc = cst[64:96, 1:2]
        for which, off, dst in (('s', 0.5, sn_t), ('c', 0.75, cs_t)):
            vop(lambda e, off=off: e.tensor_scalar(out=b.ap, in0=a.ap, scalar1=invc.ap, scalar2=off, op0=ALU.mult, op1=ALU.add), [a, invc], [b])
            vcopy(ki, b)
            kf = mem.view(T0, 2048, F32, 64, 96)
            vop(lambda e: e.tensor_copy(out=kf.ap, in_=ki.ap), [ki], [kf])
            vtt(b, b, kf, ALU.subtract)
            vop(lambda e: e.tensor_single_scalar(out=kf.ap, in_=b.ap, scalar=0.0, op=ALU.is_lt), [b], [kf])
            vtt(b, b, kf, ALU.add)
            vop(lambda e: e.tensor_scalar(out=b.ap, in0=b.ap, scalar1=-0.5, scalar2=6.28318, op0=ALU.add, op1=ALU.mult), [b], [b])
            aop(lambda e: e.activation(out=kf.ap, in_=b.ap, func=AF.Sin), [b], [kf])
            if which == 's':
                vop(lambda e, dst=dst: e.tensor_scalar(out=dst.ap, in0=kf.ap, scalar1=sgnc.ap, scalar2=None, op0=ALU.mult), [kf, sgnc], [dst])
            else:
                vcopy(dst, kf)

    rope_tables()
    dump('cs', cs_t, [32, 2048], BF16)
    dump('sn', sn_t, [32, 2048], BF16)

    def range_sin(dst, arg, tmp_i, tmp_f, np_, n):
        vop(lambda e: e.tensor_scalar(out=arg.ap, in0=arg.ap, scalar1=1.0 / (2 * math.pi), scalar2=16.5, op0=ALU.mult, op1=ALU.add), [arg], [arg])
        vcopy(tmp_i, arg)
        vcopy(tmp_f, tmp_i)
        vtt(arg, arg, tmp_f, ALU.subtract)
        vop(lambda e: e.tensor_single_scalar(out=tmp_f.ap, in_=arg.ap, scalar=0.0, op=ALU.is_lt), [arg], [tmp_f])
        vtt(arg, arg, tmp_f, ALU.add)
        vop(lambda e: e.tensor_scalar(out=arg.ap, in0=arg.ap, scalar1=-0.5, scalar2=6.28318, op0=ALU.add, op1=ALU.mult), [arg], [arg])
        aop(lambda e: e.activation(out=dst.ap, in_=arg.ap, func=AF.Sin), [arg], [dst])

    def filter_prologue(preload=None):
        S.tag = 'prologue'
        o = X0
        zemb = mem.view(o, 2048, F32, 0, 33); o += 8192
        hA = mem.view(o, 2048, F32, 0, 64); o += 8192
        hB = mem.view(o, 2048, F32, 0, 64); o += 8192
        arg = mem.view(o, 2048, F32, 0, 64); o += 8192
        ti = mem.view(o, 2048, I32, 0, 64); o += 8192
        tf = mem.view(o, 2048, F32, 0, 64); o += 8192
        dec = [mem.view(o + 4096 * i, 1024, F32) for i in range(2)]; o += 8192
        filt = mem.view(o, 1024, F32); o += 4096
        assert o <= HT0
        o = HT0
        tfs = [mem.view(o + 8192 * i, 4096, BF16).r("p (q a i) -> p q a i", q=4, a=8) for i in range(2)]; o += 16384
        stg = [mem.view(o + 8192 * i, 2048, BF16).r("p (q c) -> p q c", q=4) for i in range(2)]; o += 16384
        ocs = mem.view(o, 512, F32); o += 2048
        oss = mem.view(o, 512, F32); o += 2048
        bt = [mem.view(o + 2048 * i, 512, F32) for i in range(2)]; o += 4096
        PL = []
        for l in range(nlayers):
            d = {}
            d['w1'] = mem.view(o, 64, F32, 0, 33); o += 256
            d['w2'] = mem.view(o, 64, F32, 0, 64); o += 256
            d['w3'] = mem.view(o, 64, F32, 0, 64); o += 256
            d['w4'] = mem.view(o, 1024, F32, 0, 64); o += 4096
            d['skipB'] = mem.view(o, 512, F32); o += 2048
            d['skipA'] = [mem.view(o + 2048 * i, 512, F32) for i in range(2)]; o += 4096
            d['fpo'] = o; o += 16384
            d['fmo'] = o; o += 16384
            PL.append(d)
        assert o <= CS0, o
        S.dma('sync', lambda e: e.dma_start(out=zemb.ap, in_=zemb_d), 'f0', writes=[zemb])

        def fpa(l, r, a):
            return mem.view(PL[l]['fpo'] + r * 8192 + a * 1024, 512, BF16)

        def fma(l, r, a):
            return mem.view(PL[l]['fmo'] + r * 8192 + a * 1024, 512, BF16)

        for l in range(nlayers):
            d = PL[l]
            w1, w2, w3, w4, skipB, skipA = d['w1'], d['w2'], d['w3'], d['w4'], d['skipB'], d['skipA']
            S.dma('sync', [lambda e, w1=w1, l=l: e.dma_start(out=w1.ap, in_=fw1_d[l]),
                           lambda e, w2=w2, l=l: e.dma_start(out=w2.ap, in_=fw2_d[l]),
                           lambda e, w3=w3, l=l: e.dma_start(out=w3.ap, in_=fw3_d[l]),
                           lambda e, w4=w4, l=l: e.dma_start(out=w4.ap, in_=fw4_d[l]),
                           lambda e, skipB=skipB, l=l: e.dma_start(out=skipB.ap, in_=skip_d[l:l + 1, :].partition_broadcast(128))],
                  ('f1', l), writes=[w1, w2, w3, w4, skipB])
            cl = colsv[l]
            srcs = [(w1, zemb), (w2, hA), (w3, hB)]
            dsts = [hA, hB, hA]
            for i in range(3):
                wgt, src = srcs[i]
                pb = 4 * (i % 2)
                for tcn in range(4):
                    mm(PS(pb + tcn)[0:64, :], wgt, src[:, tcn * 512:(tcn + 1) * 512], True, True)
                bcol = cl[0:64, 52 + 2 * i:53 + 2 * i]
                fcol = cl[0:64, 53 + 2 * i:54 + 2 * i]
                pin = V(pt[0:64, pb:pb + 4, :], [('ps', pb + j) for j in range(4)])
                vop(lambda e, pin=pin, bcol=bcol, fcol=fcol: e.tensor_scalar(
                    out=arg.ap.rearrange("p (a c) -> p a c", a=4), in0=pin.ap, scalar1=bcol.ap, scalar2=fcol.ap,
                    op0=ALU.add, op1=ALU.mult), [pin, bcol, fcol], [arg])
                range_sin(dsts[i], arg, ti, tf, 64, 2048)
            h3 = hA
            cnt = 0
            for r in range(2):
                for a in range(8):
                    dv = dec[cnt % 2]
                    S.dma('sync', lambda e, dv=dv, a=a, r=r: e.dma_start(out=dv.ap, in_=decay_d[256 * a + r:256 * a + 256:2, :]), ('dec', cnt % 2), writes=[dv])
                    pb = 2 * (cnt % 2)
                    cnt += 1
                    hv = V(h3.ap[:, 256 * a + r:256 * a + 256:2], h3.keys)
                    for hlf in range(2):
                        mm(PS(pb + hlf), hv, w4[:, hlf * 512:(hlf + 1) * 512], True, True)
                    pin = PS(pb, 2)
                    vop(lambda e, pin=pin, dv=dv: e.tensor_tensor(out=filt.ap.rearrange("p (a c) -> p a c", a=2), in0=pin.ap,
                                                                 in1=dv.ap.rearrange("p (a c) -> p a c", a=2), op=ALU.mult), [pin, dv], [filt])
                    if a == 0 and r == 0:
                        vop(lambda e: e.memset(filt.ap[0:1, 512:1024], 0.0), [], [filt])
                    vtt(fpa(l, r, a), filt[:, 0:512], filt[:, 512:1024], ALU.add)
                    vtt(fma(l, r, a), filt[:, 0:512], filt[:, 512:1024], ALU.subtract)
            for i in range(2):
                ac = cst[:, 2 + i:3 + i]
                vop(lambda e, i=i, ac=ac, skipA=skipA, skipB=skipB: e.tensor_scalar(out=skipA[i].ap, in0=skipB.ap, scalar1=ac.ap, scalar2=None, op0=ALU.mult), [skipB, ac], [skipA[i]])
            pn = PS(6)[0:1, :]
            for a in range(8):
                mm(pn, altc, fpa(l, 0, a), a == 0, a == 7)
            pn2 = PS(7)[0:1, :]
            for a in range(8):
                mm(pn2, altc, fma(l, 1, a), a == 0, a == 7)
            vtt(bt[0][0:1, :], pn, skipB[0:1, :], ALU.add)
            vop(lambda e, l=l: e.tensor_scalar(out=P1k[l][0].ap, in0=bt[0].ap[0:1, :], scalar1=2.0 / 4096, scalar2=None, op0=ALU.mult), [bt[0]], [P1k[l][0]])
            vop(lambda e, l=l, pn2=pn2: e.tensor_scalar(out=P1k[l][1].ap, in0=pn2.ap, scalar1=2.0 / 4096, scalar2=None, op0=ALU.mult), [pn2], [P1k[l][1]])
        if preload is not None:
            preload()
        cnt = 0
        for j in range(8):
            sl = j % 2
            S.dma('gpsimd', lambda e, sl=sl, j=j: e.dma_start(out=tfs[sl].ap.rearrange("p q a i -> p (q a i)"), in_=tfwd_d[j]), ('tfwd', sl), writes=[tfs[sl]])
            for l in range(nlayers):
                skipA = PL[l]['skipA']
                b0 = 4 * (cnt % 2)
                sg_ = stg[cnt % 2]
                cnt += 1
                pec, pes, poc, pos_ = PS(b0), PS(b0 + 1), PS(b0 + 2), PS(b0 + 3)
                for (pp, q, fn, r) in ((pec, 0, fpa, 0), (pes, 1, fma, 0), (poc, 2, fpa, 1), (pos_, 3, fma, 1)):
                    for a in range(8):
                        mm(pp, tfs[sl][:, q, a, :], fn(l, r, a), a == 0, a == 7)
                acopy(ocs, poc)
                acopy(oss, pos_)
                ai = 0 if j == 0 else 1
                ac = cst[:, 2 + ai:3 + ai]
                sk = skipA[ai]
                vtt(bt[0], pec, ocs, ALU.add)
                vop(lambda e, ac=ac, sk=sk, sg_=sg_: e.scalar_tensor_tensor(out=sg_.ap[:, 0, :], in0=bt[0].ap, scalar=ac.ap, in1=sk.ap, op0=ALU.mult, op1=ALU.add), [bt[0], ac, sk], [sg_])
                vtt(bt[1], pes, oss, ALU.add)
                vop(lambda e, ac=ac, sg_=sg_: e.tensor_scalar(out=sg_.ap[:, 1, :], in0=bt[1].ap, scalar1=ac.ap, scalar2=None, op0=ALU.mult), [bt[1], ac], [sg_])
                vtt(bt[0], pec, ocs, ALU.subtract)
                vop(lambda e, ac=ac, sk=sk, sg_=sg_: e.scalar_tensor_tensor(out=sg_.ap[:, 2, :], in0=bt[0].ap, scalar=ac.ap, in1=sk.ap, op0=ALU.mult, op1=ALU.add), [bt[0], ac, sk], [sg_])
                vtt(bt[1], oss, pes, ALU.subtract)
                vop(lambda e, ac=ac, sg_=sg_: e.tensor_scalar(out=sg_.ap[:, 3, :], in0=bt[1].ap, scalar1=ac.ap, scalar2=None, op0=ALU.mult), [bt[1], ac], [sg_])
                S.dma('sync', lambda e, sg_=sg_, j=j, l=l: e.dma_start(out=pspec_d[l, j], in_=sg_.ap.rearrange("p q c -> p (q c)")),
                      ('pspec_w', (cnt - 1) % 2), reads=[sg_], writes=[('pspec', l, j)])

    psr = RR([0, 1, 2, 3, 4, 5, 6, 7])
    SCALE = 96.0 ** -0.5

    def load_x(b, tiles=range(16)):
        for t in tiles:
            S.dma('sync', lambda e, t=t: e.dma_start(out=Xt[t].ap, in_=x_d[b, t * 128:(t + 1) * 128, :]), ('x', t), writes=[Xt[t]])

    def rms_stats_begin(tmp0):
        blk = mem.view(tmp0 + 2048, 256, F32)
        vop(lambda e: e.memset(blk.ap, 0.0), [], [blk])
        jk = mem.view(tmp0, 1024, BF16)
        vop(lambda e: e.memset(jk.ap[:, 0:2], 0.0), [], [jk])
        return mem.view(tmp0 + 2048, 64, F32), blk, jk

    def rms_stats_act(ctx):
        stv, blk, jk = ctx
        for g in range(4):
            for t in range(4 * g, 4 * g + 4):
                S.op('scalar', lambda e, t=t: e.activation(out=jk.ap, in_=Xt[t].ap, func=AF.Square, accum_out=stv.ap[:, t:t + 1]),
                     reads=[Xt[t], blk], writes=[('ss', t), jk])
            ssk = [('ss', t) for t in range(4 * g, 4 * g + 4)]
            S.op('scalar', lambda e, g=g: e.activation(out=stv.ap[:, 32 + 4 * g:36 + 4 * g], in_=stv.ap[:, 4 * g:4 * g + 4], func=AF.Sqrt, scale=1.0 / 1024, bias=epsc.ap),
                 reads=ssk + [blk, epsc], writes=[('sd', g)])

    def rms_rstd(ctx, g):
        stv, blk, jk = ctx
        S.op('vector', lambda e: e.reciprocal(out=stv.ap[:, 48 + 4 * g:52 + 4 * g], in_=stv.ap[:, 32 + 4 * g:36 + 4 * g]), reads=[('sd', g), blk], writes=[('rstd', g)])

    gB2 = mem.view(TB0 + 20480, 1024, F32)

    def load_gain(gidx, dst=None):
        dst = gB if dst is None else dst
        S.dma('sync', lambda e: e.dma_start(out=dst.ap, in_=ng_d[gidx:gidx + 1, :].partition_broadcast(128)), ('gB', 0 if dst is gB else 1), writes=[dst])

    def norm_to_hT(gB, tmp0):
        S.tag = 'norm'
        hn = [mem.view(tmp0 + 4096 + 2048 * j, 1024, BF16) for j in range(4)]
        ctx = rms_stats_begin(tmp0)
        stv, blk, _ = ctx
        rms_stats_act(ctx)
        for t in range(16):
            j = t % 4
            if t % 4 == 0:
                rms_rstd(ctx, t // 4)
            S.op('vector', lambda e, t=t, j=j: e.scalar_tensor_tensor(out=hn[j].ap, in0=Xt[t].ap, scalar=stv.ap[:, 48 + t:49 + t], in1=gB.ap, op0=ALU.mult, op1=ALU.mult),
                 reads=[Xt[t], ('rstd', t // 4), blk, gB], writes=[hn[j]])
            pb = psr()
            pv = PSB(pb)
            for k in range(8):
                tr(pv[:, k * 128:(k + 1) * 128], hn[j][:, k * 128:(k + 1) * 128], ident)
            if t < 8 or t % 2 == 0:
                vcopy(HT_tile(t), pv.r("p (k t) -> p k t", k=8))
            else:
                acopy(HT_tile(t), pv.r("p (k t) -> p k t", k=8))

    NT0 = TB0 + 24576

    def hn_slot(t):
        j = t % 4
        return mem.view(NT0 + 4096 + 2048 * j, 1024, BF16) if j < 2 else mem.view(TB0 + 16384 + 2048 * (j - 2), 1024, BF16)

    def norm_group_a(ctx, g, gBv, mode, b=None):
        tg = S.tag
        S.tag = 'norm_i'
        stv, blk, jk = ctx
        for t in range(4 * g, 4 * g + 4):
            S.op('scalar', lambda e, t=t: e.activation(out=jk.ap, in_=Xt[t].ap, func=AF.Square, accum_out=stv.ap[:, t:t + 1]),
                 reads=[Xt[t], blk], writes=[('ss', t), jk])
        ssk = [('ss', t) for t in range(4 * g, 4 * g + 4)]
        S.op('scalar', lambda e, g=g: e.activation(out=stv.ap[:, 32 + 4 * g:36 + 4 * g], in_=stv.ap[:, 4 * g:4 * g + 4], func=AF.Sqrt, scale=1.0 / 1024, bias=epsc.ap),
             reads=ssk + [blk, epsc], writes=[('sd', g)])
        rms_rstd(ctx, g)
        for t in range(4 * g, 4 * g + 4):
            if mode == 'hT':
                hn = hn_slot(t)
                S.op('vector', lambda e, t=t, hn=hn: e.scalar_tensor_tensor(out=hn.ap, in0=Xt[t].ap, scalar=stv.ap[:, 48 + t:49 + t], in1=gBv.ap, op0=ALU.mult, op1=ALU.mult),
                     reads=[Xt[t], ('rstd', g), blk, gBv], writes=[hn])
            else:
                stg = mem.view(R10 + 20480 + 4096 * (t % 3), 1024, F32)
                S.op('vector', lambda e, t=t, stg=stg: e.scalar_tensor_tensor(out=stg.ap, in0=Xt[t].ap, scalar=stv.ap[:, 48 + t:49 + t], in1=gBv.ap, op0=ALU.mult, op1=ALU.mult),
                     reads=[Xt[t], ('rstd', g), blk, gBv], writes=[stg])
                S.dma('sync', lambda e, t=t, stg=stg: e.dma_start(out=out_d[b, t * 128:(t + 1) * 128, :], in_=stg.ap), ('out', t % 3), reads=[stg], writes=[('OUT', b, t)])
        S.tag = tg

    def norm_group_b(g, pbank=7):
        tg = S.tag
        S.tag = 'norm_i'
        for t in range(4 * g, 4 * g + 4):
            hn = hn_slot(t)
            pv = PSB(pbank)
            for k in range(8):
                tr(pv[:, k * 128:(k + 1) * 128], hn[:, k * 128:(k + 1) * 128], ident)
            if t % 2 == 0:
                vcopy(HT_tile(t), pv.r("p (k t) -> p k t", k=8))
            else:
                acopy(HT_tile(t), pv.r("p (k t) -> p k t", k=8))
        S.tag = tg

    def hyena(b, l):
        S.tag = 'hy_conv'
        cl = colsv[l]
        us = [mem.view(R10 + 9216 * i, 2052, F32) for i in range(2)]
        t1s = [mem.view(R10 + 18432, 2048, F32), mem.view(TB0 + 12288, 2048, F32)]
        zT = mem.view(R10 + 26624, 2048, BF16)
        x1c = mem.view(TB0 + 4096, 2048, F32)
        whs = [mem.view(TB0 + 2048 * i, 1024, BF16).r("p (k c) -> p k c", k=8) for i in range(2)]
        ztok = mem.view(ZY0, 16 * 512, BF16).r("p (r a c) -> p r a c", r=2, a=8)
        x0T = [mem.view(OT0 + 4096 * c, 2048, BF16) for c in range(4)]
        for u in us:
            vop(lambda e, u=u: e.memset(u.ap[:, 0:1], 0.0), [], [u])
            vop(lambda e, u=u: e.memset(u.ap[:, 2049:2050], 0.0), [], [u])
        wcnt = [0]
        cbank = RR([0, 1, 2, 3, 4, 5])
        tbank = RR([6, 7])

        def conv(j, dst):
            sl = wcnt[0] % 2
            wcnt[0] += 1
            u = us[sl]
            t1 = t1s[sl]
            if dst is None:
                dst = t1
            wdma(whs[sl], w_in_d[l, :, 416 + 128 * j:416 + 128 * (j + 1)].rearrange("(k p) c -> p k c", p=128), ('whs', sl))
            w0, w1, w2, bb = [cl[:, 4 * j + i:4 * j + i + 1] for i in range(4)]
            for tcn in range(4):
                pin = PS(cbank())
                for k in range(8):
                    mm(pin, whs[sl][:, k, :], HTs(k, tcn * 512, 512), k == 0, k == 7)
                uv = V(u.ap[:, 1 + tcn * 512:1 + (tcn + 1) * 512], mem.view(R10 + 9216 * sl + 4 + tcn * 2048, 512, F32).keys)
                tv = V(t1.ap[:, tcn * 512:(tcn + 1) * 512], t1.keys)
                aop(lambda e, uv=uv, pin=pin: e.activation(out=uv.ap, in_=pin.ap, func=AF.Copy), [pin], [uv])
                aop(lambda e, tv=tv, pin=pin: e.activation(out=tv.ap, in_=pin.ap, func=AF.Identity, scale=w1.ap, bias=bb.ap), [pin, w1, bb], [tv])
            vop(lambda e: e.scalar_tensor_tensor(out=t1.ap, in0=u.ap[:, 0:2048], scalar=w0.ap, in1=t1.ap, op0=ALU.mult, op1=ALU.add), [u, w0, t1], [t1])
            vop(lambda e: e.scalar_tensor_tensor(out=dst.ap, in0=u.ap[:, 2:2050], scalar=w2.ap, in1=t1.ap, op0=ALU.mult, op1=ALU.add), [u, w2, t1], [dst])
            return dst

        def ztrans(cc):
            for r in range(2):
                pv = PSB(tbank())
                for a in range(8):
                    tr(pv[:, a * 128:(a + 1) * 128], V(zT.ap[:, 256 * a + r:256 * a + 256:2], zT.keys), ident)
                keys = []
                for a in range(8):
                    keys += mem.view(ZY0 + r * 8192 + a * 1024 + cc * 256, 128, BF16).keys
                dstv = V(ztok.ap[:, r, :, cc * 128:(cc + 1) * 128], keys)
                anycopy(dstv, pv.r("p (a c) -> p a c", a=8))

        for cc in range(4):
            conv(4 + cc, x1c)
            if cc > 0:
                ztrans(cc - 1)
            vc = conv(8 + cc, None)
            vtt(zT, vc, x1c, ALU.mult)
        ztrans(3)
        dump('ztok', V(ztok.ap, mem.view(ZY0, 8192, BF16).keys), [128, 2, 8, 512], BF16)

        S.tag = 'hy_fwd'
        AB = [mem.view(R10 + 8192 * i, 4096, BF16).r("p (a c) -> p a c", a=8) for i in range(4)]

        def zt(r, a):
            return mem.view(ZY0 + r * 8192 + a * 1024, 512, BF16)

        def abv(i, a, c0=0, n=512):
            return V(AB[i].ap[:, a, c0:c0 + n], mem.view(R10 + 8192 * i + a * 1024 + c0 * 2, n, BF16).keys)

        tfs = [mem.view(TB0 + 8192 * i, 4096, BF16).r("p (q a i) -> p q a i", q=4, a=8) for i in range(2)]
        Pt = [mem.view(TB0 + 16384 + 8192 * i, 2048, BF16).r("p (q c) -> p q c", q=4) for i in range(2)]
        tm = [mem.view(OT0 + 2048 * i, 512, BF16) for i in range(8)]
        for j in range(8):
            sl = j % 2
            S.dma('gpsimd', lambda e, sl=sl, j=j: e.dma_start(out=tfs[sl].ap.rearrange("p q a i -> p (q a i)"), in_=tfwd_d[j]), ('tfwd', sl), writes=[tfs[sl]])
            S.dma('sync', lambda e, sl=sl, j=j: e.dma_start(out=Pt[sl].ap.rearrange("p q c -> p (q c)"), in_=pspec_d[l, j]),
                  ('pspec_r', sl), reads=[('pspec', l, j)], writes=[Pt[sl]])
            b0 = 4 * sl
            pec, pes, poc, pos_ = PS(b0), PS(b0 + 1), PS(b0 + 2), PS(b0 + 3)
            for (pp, q, r) in ((pec, 0, 0), (pes, 1, 0), (poc, 2, 1), (pos_, 3, 1)):
                for a in range(8):
                    mm(pp, tfs[sl][:, q, a, :], zt(r, a), a == 0, a == 7)
            ocs, oss, xc, xcm, xs, t2, ecs, ess = tm
            pr, qs, prm, qsm = [Pt[sl][:, q, :] for q in range(4)]
            acopy(ecs, pec)
            acopy(ocs, poc)
            acopy(ess, pes)
            acopy(oss, pos_)
            vtt(xc, ecs, ocs, ALU.add)
            vtt(xcm, ecs, ocs, ALU.subtract)
            vtt(xs, ess, oss, ALU.add)
            vtt(oss, oss, ess, ALU.subtract)
            xsm = oss
            vtt(ocs, xc, pr, ALU.mult)
            vtt(t2, xs, qs, ALU.mult)
            vtt(ocs, ocs, t2, ALU.subtract)
            vtt(xs, xs, pr, ALU.mult)
            vtt(xc, xc, qs, ALU.mult)
            vtt(xs, xs, xc, ALU.add)
            vtt(xc, xcm, prm, ALU.mult)
            vtt(t2, xsm, qsm, ALU.mult)
            vtt(xc, xc, t2, ALU.subtract)
            vtt(xsm, xsm, prm, ALU.mult)
            vtt(xcm, xcm, qsm, ALU.mult)
            vtt(xsm, xsm, xcm, ALU.add)
            vtt(abv(0, j), ocs, xc, ALU.add)
            vtt(abv(2, j), ocs, xc, ALU.subtract)
            vtt(abv(1, j), xs, xsm, ALU.subtract)
            vtt(abv(3, j), xs, xsm, ALU.add)
        px, px2 = PS(0)[0:1, :], PS(1)[0:1, :]
        for a in range(8):
            mm(px, altc, zt(0, a), a == 0, a == 7)
        for a in range(8):
            mm(px2, altc, zt(1, a), a == 0, a == 7)
        k0, k1, k2 = tm[0][0:1, :], tm[1][0:1, :], tm[2][0:1, :]
        vtt(k0, px, P1k[l][0], ALU.mult)
        vtt(k1, px2, P1k[l][1], ALU.mult)
        vtt(G1k[0], k0, k1, ALU.subtract)
        vtt(k0, px2, P1k[l][0], ALU.mult)
        vtt(k1, px, P1k[l][1], ALU.mult)
        vtt(G1k[1], k0, k1, ALU.add)

        S.tag = 'hy_inv'
        yT = [mem.view(ZY0 + 4096 * c, 2048, BF16) for c in range(4)]
        tis = [mem.view(TB0 + 16384 * i, 8192, BF16).r("p (q a i) -> p q a i", q=4, a=8) for i in range(2)]
        for g in range(4):
            sl = g % 2
            S.dma('gpsimd', lambda e, sl=sl, g=g: e.dma_start(out=tis[sl].ap.rearrange("p q a i -> p (q a i)"), in_=tinv_d[g]), ('tinv', sl), writes=[tis[sl]])
            for cc in range(4):
                for r in range(2):
                    po = PS(psr())[:, 0:256]
                    for a in range(8):
                        mm(po, abv(2 * r, a, cc * 128, 128), tis[sl][:, 2 * r, a, :], a == 0, False, inc=False)
                    for a in range(8):
                        mm(po, abv(2 * r + 1, a, cc * 128, 128), tis[sl][:, 2 * r + 1, a, :], False, False, inc=False)
                    mm(po, G1k[r][:, cc * 128:(cc + 1) * 128], altrow, False, True)
                    yv = V(yT[cc].ap[:, g * 512 + r:(g + 1) * 512:2], mem.view(ZY0 + 4096 * cc + g * 1024, 512, BF16).keys)
                    acopy(yv, po)
        S.tag = 'hy_conv'
        for cc in range(4):
            conv(cc, x0T[cc])
            vtt(yT[cc], yT[cc], x0T[cc], ALU.mult)
        dump('yT', mem.view(ZY0, 8192, BF16), [128, 8192], BF16)

    def attention(b, l):
        S.tag = 'latents'
        cl = colsv[l]
        qlatT = [mem.view(TB0 + 4096 * k, 2048, BF16) for k in range(2)]
        kvlatT = mem.view(TB0 + 8192, 2048, BF16)
        kpeT = mem.view(TB0 + 12288, 2048, BF16, 64, 96)
        wqv = mem.view(TB0 + 16384, 2 * 2 * 768, BF16).r("p (k v c) -> p k v c", k=2, v=2)
        wkv_ = mem.view(TB0 + 22528, 512, BF16)
        wvv = mem.view(TB0 + 23552, 512, BF16)
        sq = [mem.view(TB0 + 24576, 3 * 512, BF16).r("p (a c) -> p a c", a=3), mem.view(R10 + 20480, 3 * 512, BF16).r("p (a c) -> p a c", a=3)]
        rq = [mem.view(TB0 + 27648 + 2048 * i, 512, F32) for i in range(2)]
        wlat = mem.view(R10, 8 * 384, BF16).r("p (k c) -> p k c", k=8)
        wkp = mem.view(R10 + 6144, 8 * 192, BF16).r("p (k v c) -> p k v c", k=8, v=2)
        tmpA = mem.view(R10 + 12288, 512, F32)
        tmpB = mem.view(R10 + 14336, 512, F32)
        tmpA2 = mem.view(R10 + 16384, 512, F32)
        tmpB2 = mem.view(R10 + 18432, 512, F32)
        wdma(wlat, w_in_d[l, :, 0:384].rearrange("(k p) c -> p k c", p=128), 'wlat')
        wdma(wkp, wkpe_d[l].rearrange("(k p) v c -> p k v c", p=128), 'wkp')
        wdma(wqv, wq_d[l].rearrange("(k p) v c -> p k v c", p=128), 'wq')
        wdma(wkv_, wk_d[l], 'wk')
        wdma(wvv, wv_d[l], 'wv')
        gq = [cl[:, 48 + k:49 + k] for k in range(2)]
        gk = cl[:, 50:51]
        for tcn in range(4):
            tsl = slice(tcn * 512, (tcn + 1) * 512)
            pa, pbb = PS(2 * (tcn % 2)), PS(2 * (tcn % 2) + 1)
            for k in range(8):
                mm(pa[0:96, :], wkp[:, k, 0, :], HTs(k, tcn * 512, 512), k == 0, k == 7)
            for k in range(8):
                mm(pbb[0:96, :], wkp[:, k, 1, :], HTs(k, tcn * 512, 512), k == 0, k == 7)
            csv = V(cs_t.ap[:, tsl], cs_t.keys)
            snv = V(sn_t.ap[:, tsl], sn_t.keys)
            tA = tmpA if tcn % 2 == 0 else tmpA2
            tB = tmpB if tcn % 2 == 0 else tmpB2
            vtt(tA[64:96, :], pa[64:96, :], csv, ALU.mult)
            vtt(tB[64:96, :], pbb[64:96, :], snv, ALU.mult)
            dst = V(kpeT.ap[:, tsl], mem.view(TB0 + 12288 + tcn * 1024, 512, BF16, 64, 96).keys)
            vtt(dst, tA[64:96, :], tB[64:96, :], ALU.add)
        for tcn in range(4):
            tsl = slice(tcn * 512, (tcn + 1) * 512)
            b0 = 3 * (tcn % 2)
            pq = [PS(b0), PS(b0 + 1)]
            pk = PS(b0 + 2)
            for k in range(8):
                mm(pq[0], wlat[:, k, 0:128], HTs(k, tcn * 512, 512), k == 0, k == 7)
            for k in range(8):
                mm(pq[1], wlat[:, k, 128:256], HTs(k, tcn * 512, 512), k == 0, k == 7)
            for k in range(8):
                mm(pk, wlat[:, k, 256:384], HTs(k, tcn * 512, 512), k == 0, k == 7)
            sqv = sq[tcn % 2]
            for i, p in enumerate([pq[0], pq[1], pk]):
                aop(lambda e, i=i, p=p, sqv=sqv: e.activation(out=sqv.ap[:, i, :], in_=p.ap, func=AF.Square), [p], [sqv])
            pss, psk = PS(6), PS(7)
            mm(pss, ones, sqv[:, 0, :], True, False, inc=False)
            mm(pss, ones, sqv[:, 1, :], False, True)
            mm(psk, ones, sqv[:, 2, :], True, True)
            for (p, n, r) in ((pss, 256, rq[0]), (psk, 128, rq[1])):
                aop(lambda e, p=p, n=n, r=r: e.activation(out=r.ap, in_=p.ap, func=AF.Ln, scale=1.0 / n, bias=epsc.ap), [p, epsc], [r])
                aop(lambda e, r=r: e.activation(out=r.ap, in_=r.ap, func=AF.Exp, scale=-0.5), [r], [r])
            for k in range(2):
                dst = V(qlatT[k].ap[:, tsl], mem.view(TB0 + 4096 * k + tcn * 1024, 512, BF16).keys)
                vop(lambda e, k=k, dst=dst, pq=pq: e.scalar_tensor_tensor(out=dst.ap, in0=pq[k].ap, scalar=gq[k].ap, in1=rq[0].ap, op0=ALU.mult, op1=ALU.mult),
                    [pq[k], gq[k], rq[0]], [dst])
            dst = V(kvlatT.ap[:, tsl], mem.view(TB0 + 8192 + tcn * 1024, 512, BF16).keys)
            vop(lambda e, dst=dst, pk=pk: e.scalar_tensor_tensor(out=dst.ap, in0=pk.ap, scalar=gk.ap, in1=rq[1].ap, op0=ALU.mult, op1=ALU.mult),
                [pk, gk, rq[1]], [dst])
        dump('qlatT', mem.view(TB0, 4096, BF16), [128, 4096], BF16)
        dump('kvlatT', kvlatT, [128, 2048], BF16)
        dump('kpeT', kpeT, [32, 2048], BF16)

        QT = [mem.view(R10 + 4096 * i, 2048, BF16) for i in range(2)]
        KT = [mem.view(R10 + 8192 + 4096 * i, 2048, BF16) for i in range(2)]
        VA = [mem.view(R10 + 16384 + 4096 * i, 2048, BF16).r("p (a c) -> p a c", a=16) for i in range(2)]
        PT = [mem.view(R10 + 24576 + 2048 * i, 1024, BF16).r("p (a c) -> p a c", a=2) for i in range(3)]
        tmpA = mem.view(TB0 + 24576, 512, F32)
        tmpB = mem.view(R10 + 30720, 512, F32)
        OT = [mem.view(OT0 + 4096 * j, 2048, BF16) for j in range(4)]
        rden = [mem.view(TB0 + 27648 + 2048 * i, 512, F32) for i in range(2)]
        vop(lambda e: e.memset(VA[0].ap[:, :, 64:128], 1.0), [], [VA[0]])
        vop(lambda e: e.memset(VA[1].ap[:, :, 0:64], 1.0), [], [VA[1]])
        ptr = RR([0, 1, 2])
        s2r = RR([0, 2])
        orr = RR([4, 5])
        def qkv_pieces(h):
            sl = h % 2
            pieces = []

            def q_piece(tcn):
                S.tag = 'qkv'
                tsl = slice(tcn * 512, (tcn + 1) * 512)
                pa, pbb = PS(6), PS(7)
                qv = [V(qlatT[k].ap[:, tsl], mem.view(TB0 + 4096 * k + tcn * 1024, 512, BF16).keys) for k in range(2)]
                for k in range(2):
                    mm(pa[0:96, :], wqv[:, k, 0, h * 96:(h + 1) * 96], qv[k], k == 0, k == 1)
                for k in range(2):
                    mm(pbb[0:96, :], wqv[:, k, 1, h * 96:(h + 1) * 96], qv[k], k == 0, k == 1)
                qk = mem.view(R10 + 4096 * sl + tcn * 1024, 512, BF16).keys
                vcopy(V(QT[sl].ap[0:64, tsl], qk), pa[0:64, :])
                csv = V(cs_t.ap[:, tsl], cs_t.keys)
                snv = V(sn_t.ap[:, tsl], sn_t.keys)
                vtt(tmpA[64:96, :], pa[64:96, :], csv, ALU.mult)
                vtt(tmpB[64:96, :], pbb[64:96, :], snv, ALU.mult)
                vtt(V(QT[sl].ap[64:96, tsl], qk), tmpA[64:96, :], tmpB[64:96, :], ALU.add)

            def k_piece(tp_):
                S.tag = 'qkv'
                for j in range(2):
                    tcn = tp_ * 2 + j
                    tsl = slice(tcn * 512, (tcn + 1) * 512)
                    pkk = PS(6 + j)
                    mm(pkk[0:64, :], wkv_[:, h * 64:(h + 1) * 64], V(kvlatT.ap[:, tsl], mem.view(TB0 + 8192 + tcn * 1024, 512, BF16).keys), True, True)
                    kk = mem.view(R10 + 8192 + 4096 * sl + tcn * 1024, 512, BF16).keys
                    vcopy(V(KT[sl].ap[0:64, tsl], kk), pkk[0:64, :])
                if tp_ == 1:
                    vcopy(KT[sl][64:96, :], kpeT)

            def v_piece():
                S.tag = 'qkv'
                pv2 = PS(6, 2)
                for kt in range(16):
                    pvv = V(pt[:, 6 + kt // 8, (kt % 8) * 64:(kt % 8 + 1) * 64], [('ps', 6 + kt // 8)])
                    mm(pvv, V(kvlatT.ap[:, kt * 128:(kt + 1) * 128], mem.view(TB0 + 8192 + kt * 256, 128, BF16).keys), wvv[:, h * 64:(h + 1) * 64], True, True)
                voff = 0 if sl == 0 else 64
                vcopy(V(VA[sl].ap[:, :, voff:voff + 64], VA[sl].keys), V(pt[:, 6:8, :].rearrange("p b (a c) -> p (b a) c", c=64), pv2.keys))

            for tcn in range(4):
                pieces.append(lambda tcn=tcn: q_piece(tcn))
            for tp_ in range(2):
                pieces.append(lambda tp_=tp_: k_piece(tp_))
            pieces.append(v_piece)
            return pieces

        _p0 = qkv_pieces(0)
        for p_ in (_p0[0], _p0[4], _p0[5], _p0[6]):
            p_()
        cur_pieces0 = [_p0[1], _p0[2], _p0[3]]
        dump('QT0', QT[0], [128, 2048], BF16)
        dump('KT0', KT[0], [128, 2048], BF16)
        dump('VA0', V(VA[0].ap, VA[0].keys), [128, 16, 128], BF16)
        for h in range(NH):
            sl = h % 2
            steps = [(qc, kp) for qc in range(4) for kp in range(8)]
            obank = [None]
            nxt_pieces = qkv_pieces(h + 1) if h + 1 < NH else []
            ptis = {}

            def emitS(k):
                S.tag = 'attn'
                qc, kp = steps[k]
                s2 = s2r()
                for j in range(2):
                    kt = kp * 2 + j
                    mm(PS(s2 + j), V(KT[sl].ap[0:96, kt * 128:(kt + 1) * 128], mem.view(R10 + 8192 + 4096 * sl + kt * 256, 128, BF16).keys),
                       V(QT[sl].ap[0:96, qc * 512:(qc + 1) * 512], mem.view(R10 + 4096 * sl + qc * 1024, 512, BF16).keys), True, True)
                pti = ptr()
                sin = PS(s2, 2)
                aop(lambda e, pti=pti, sin=sin: e.activation(out=PT[pti].ap, in_=sin.ap, func=AF.Exp, scale=SCALE), [sin], [PT[pti]])
                ptis[k] = pti

            def emitPV(k):
                S.tag = 'attn'
                pqc, pkp = steps[k]
                ppti = ptis[k]
                if pkp == 0:
                    obank[0] = orr()
                ob = PS(obank[0])
                for j in range(2):
                    kt = pkp * 2 + j
                    mm(ob, V(VA[sl].ap[:, kt, :], mem.view(R10 + 16384 + 4096 * sl + kt * 256, 128, BF16).keys), PT[ppti][:, j, :],
                       pkp == 0 and j == 0, pkp == 7 and j == 1)
                if pkp == 7:
                    j2 = h // 2
                    osl = slice(pqc * 512, (pqc + 1) * 512)
                    ok = mem.view(OT0 + 4096 * j2 + pqc * 1024, 512, BF16).keys
                    rd = rden[pqc % 2]
                    if sl == 0:
                        vop(lambda e, ob=ob, rd=rd: e.reciprocal(out=rd.ap[0:64, :], in_=ob.ap[64:128, :]), [ob], [rd])
                        vtt(V(OT[j2].ap[0:64, osl], ok), ob[0:64, :], rd[0:64, :], ALU.mult)
                    else:
                        vop(lambda e, ob=ob, rd=rd: e.reciprocal(out=rd.ap[64:128, :], in_=ob.ap[0:64, :]), [ob], [rd])
                        vtt(V(OT[j2].ap[64:128, osl], ok), ob[64:128, :], rd[64:128, :], ALU.mult)

            nst = len(steps)
            emitS(0)
            emitS(1)
            for k in range(nst):
                if h == 0 and k % 4 == 0 and cur_pieces0:
                    cur_pieces0.pop(0)()
                if k >= 2 and (k - 2) % 4 == 0 and nxt_pieces:
                    nxt_pieces.pop(0)()
                if k + 2 < nst:
                    emitS(k + 2)
                emitPV(k)
        dump('OT', mem.view(OT0, 8192, BF16), [128, 8192], BF16)

    def merge(b, l):
        S.tag = 'merge'
        nctx = rms_stats_begin(NT0)
        wob = RR([4, 5, 6])
        wout = mem.view(R10, 8 * 1024, BF16).r("p (k c) -> p k c", k=8)
        bun = [mem.view(R10 + 16384 + 6144 * i, 3072, BF16) for i in range(2)]
        sg = [mem.view(R10 + 28672 + 2048 * i, 512, F32) for i in range(2)]
        mg = [mem.view(TB0 + 8192 * i, 8 * 512, BF16).r("p (k c) -> p k c", k=8) for i in range(2)]
        wdma(wout, wout_d[l].rearrange("(k p) c -> p k c", p=128), 'wout')
        bc = [0]
        for tcn in range(4):
            ms = tcn % 2
            for fc in range(8):
                sl = bc[0] % 2
                bc[0] += 1
                S.dma('sync', lambda e, sl=sl, fc=fc: e.dma_start(out=bun[sl].ap, in_=mrgb_d[l, fc]), ('bun', sl), reads=[('mrgb', l, fc)], writes=[bun[sl]])
                pga, pgb, pya, pyh = PS(0), PS(1), PS(2), PS(3)
                for k in range(8):
                    mm(pga, bun[sl][:, k * 128:(k + 1) * 128], HTs(k, tcn * 512, 512), k == 0, k == 7)
                for k in range(8):
                    mm(pgb, bun[sl][:, 1024 + k * 128:1024 + (k + 1) * 128], HTs(k, tcn * 512, 512), k == 0, k == 7)
                for j in range(4):
                    mm(pya, bun[sl][:, 2048 + j * 128:2048 + (j + 1) * 128], mem.view(OT0 + 4096 * j + tcn * 1024, 512, BF16), j == 0, j == 3)
                for j in range(4):
                    mm(pyh, bun[sl][:, 2560 + j * 128:2560 + (j + 1) * 128], mem.view(ZY0 + 4096 * j + tcn * 1024, 512, BF16), j == 0, j == 3)
                aop(lambda e: e.activation(out=sg[0].ap, in_=pga.ap, func=AF.Sigmoid), [pga], [sg[0]])
                aop(lambda e: e.activation(out=sg[1].ap, in_=pgb.ap, func=AF.Sigmoid), [pgb], [sg[1]])
                vtt(sg[0], sg[0], pya, ALU.mult)
                vtt(sg[1], sg[1], pyh, ALU.mult)
                dst = V(mg[ms].ap[:, fc, :], mem.view(TB0 + 8192 * ms + fc * 1024, 512, BF16).keys)
                vtt(dst, sg[0], sg[1], ALU.add)
                if fc == 3 and tcn > 0:
                    norm_group_b(tcn - 1)
            for tt in range(4):
                t = tcn * 4 + tt
                for hf in range(2):
                    po = PS(wob())
                    for fc in range(8):
                        mm(po, V(mg[ms].ap[:, fc, tt * 128:(tt + 1) * 128], mem.view(TB0 + 8192 * ms + fc * 1024 + tt * 256, 128, BF16).keys),
                           V(wout.ap[:, fc, hf * 512:(hf + 1) * 512], mem.view(R10 + fc * 2048 + hf * 1024, 512, BF16).keys), fc == 0, fc == 7)
                    xv = V(Xt[t].ap[:, hf * 512:(hf + 1) * 512], mem.view(X0 + t * 4096 + hf * 2048, 512, F32).keys)
                    vtt(xv, xv, po, ALU.add)
            norm_group_a(nctx, tcn, gB2, 'hT')
        norm_group_b(3)

    def ffn(b, l, nxt=None, nb_next=False):
        S.tag = 'ffn'
        fbank = RR([0, 1, 2, 3, 4, 5, 6])
        nctx = rms_stats_begin(NT0) if nxt is not None else None
        act = mem.view(R10, 4 * 2048, BF16).r("p (c t) -> p c t", c=4)
        sgt = [mem.view(R10 + 16384 + 2048 * i, 512, F32) for i in range(2)]
        wsl = [mem.view(ZY0 + 24576 * i, 12288, BF16) for i in range(2)]
        for gi, (g0, gs) in enumerate(FF_GROUPS):
            sl = gi % 2
            nch = gs // 128
            tot = 16 * gs + nch * 1024
            W = wsl[sl]
            S.dma('gpsimd', lambda e, W=W, gi=gi, tot=tot: e.dma_start(out=W.ap[:, 0:tot], in_=ffn_d[l, gi, :, 0:tot]), ('ffw', sl), writes=[W])
            for ch in range(nch):
                for tcn in range(4):
                    pg, pu = PS(fbank()), PS(fbank())
                    for k in range(8):
                        mm(pg, W[:, k * gs + ch * 128:k * gs + (ch + 1) * 128], HTs(k, tcn * 512, 512), k == 0, k == 7)
                    for k in range(8):
                        mm(pu, W[:, 8 * gs + k * gs + ch * 128:8 * gs + k * gs + (ch + 1) * 128], HTs(k, tcn * 512, 512), k == 0, k == 7)
                    st_ = sgt[(ch * 4 + tcn) % 2]
                    aop(lambda e, st_=st_, pg=pg: e.activation(out=st_.ap, in_=pg.ap, func=AF.Silu), [pg], [st_])
                    dst = V(act.ap[:, ch, tcn * 512:(tcn + 1) * 512], mem.view(R10 + ch * 4096 + tcn * 1024, 512, BF16).keys)
                    vtt(dst, st_, pu, ALU.mult)
            for t in range(16):
                for hf in range(2):
                    po = PS(fbank())
                    for ch in range(nch):
                        mm(po, V(act.ap[:, ch, t * 128:(t + 1) * 128], mem.view(R10 + ch * 4096 + t * 256, 128, BF16).keys),
                           W[:, 16 * gs + ch * 1024 + hf * 512:16 * gs + ch * 1024 + (hf + 1) * 512], ch == 0, ch == nch - 1)
                    xv = V(Xt[t].ap[:, hf * 512:(hf + 1) * 512], mem.view(X0 + t * 4096 + hf * 2048, 512, F32).keys)
                    vtt(xv, xv, po, ALU.add)
                if gi == len(FF_GROUPS) - 1 and nxt is not None:
                    if nxt == 'mix':
                        if t % 4 == 1 and t // 4 > 0:
                            norm_group_b(t // 4 - 1)
                        if t % 4 == 3:
                            norm_group_a(nctx, t // 4, gB, 'hT')
                        if t == 15:
                            norm_group_b(3)
                    else:
                        g = t // 4
                        if t % 4 == 1 and g >= 2 and nb_next:
                            norm_group_b(g - 2)
                        if t % 4 == 3:
                            norm_group_a(nctx, g, gB2, 'out', b=b)
                            if nb_next:
                                load_x(b + 1, range(4 * g, 4 * g + 4))
                                if g >= 1:
                                    norm_group_a(nctx, g - 1, gB, 'hT')
            if gi == len(FF_GROUPS) - 1 and nxt == 'final' and nb_next:
                norm_group_b(2)
                norm_group_a(nctx, 3, gB, 'hT')
                norm_group_b(3)

    def final_norm(b):
        S.tag = 'final'
        gB = gB2
        tmp0 = R10
        stg = [mem.view(R10 + 8192 + 4096 * i, 1024, F32) for i in range(4)]
        ctx = rms_stats_begin(tmp0)
        stv, blk, _ = ctx
        rms_stats_act(ctx)
        for t in range(16):
            j = t % 4
            if t % 4 == 0:
                rms_rstd(ctx, t // 4)
            S.op('vector', lambda e, t=t, j=j: e.scalar_tensor_tensor(out=stg[j].ap, in0=Xt[t].ap, scalar=stv.ap[:, 48 + t:49 + t], in1=gB.ap, op0=ALU.mult, op1=ALU.mult),
                 reads=[Xt[t], ('rstd', t // 4), blk, gB], writes=[stg[j]])
            S.dma('sync', lambda e, t=t, j=j: e.dma_start(out=out_d[b, t * 128:(t + 1) * 128, :], in_=stg[j].ap), ('out', j), reads=[stg[j]], writes=[('OUT', b, t)])

    filter_prologue(lambda: load_x(0))
    for l in range(nlayers):
        for fc in range(8):
            S.dma('gpsimd', lambda e, l=l, fc=fc: e.dma_start(out=mrgb_d[l, fc], in_=mrg_d[l, fc]), ('mrgb', l, fc), writes=[('mrgb', l, fc)])

    load_gain(0)
    for b in range(nseq):
        for l in range(nlayers):
            if l == 0 and b == 0:
                norm_to_hT(gB, TB0)
            if b == 0 and l == 0:
                dump('HT', mem.view(HT0, 16384, BF16), [128, 16384], BF16)
            hyena(b, l)
            attention(b, l)
            load_gain(2 + l, gB2)
            merge(b, l)
            if b == 0 and l == 0:
                dump('Xmix', mem.view(X0, 16384, F32), [128, 16384], F32)
            last = (l + 1 == nlayers)
            if not last:
                load_gain(l + 1)
            else:
                load_gain(4, gB2)
                if b + 1 < nseq:
                    load_gain(0)
            ffn(b, l, 'final' if last else 'mix', nb_next=(last and b + 1 < nseq))
            if b == 0 and l == 0:
                dump('Xffn', mem.view(X0, 16384, F32), [128, 16384], F32)
    outs = [('OUT', b, t) for b in range(nseq) for t in range(16)]
    if dbg_out:
        outs.append('DBGOUT')
    S.wait_all('sync', outs)
    S.finalize()
    es.close()
    return nc, S


def _device_inputs(inp, nseq_total=32):
    c = _host_consts()
    w = _prep_weights(inp)
    shared = {}
    for k in ['positions', 'w_in', 'wkpe', 'wq', 'wk', 'wv', 'mrg', 'w_out', 'ffn', 'norm_g', 'cols', 'hy_skip',
              'filt_w1', 'filt_w2', 'filt_w3', 'filt_w4']:
        shared[k] = w[k]
    for k in ['tfwd', 'tinv', 'zembT', 'decay', 'cst', 'cstb', 'altrow']:
        shared[k] = c[k]
    return shared


def kernel(**inputs):
    x = np.ascontiguousarray(np.asarray(inputs['x'], dtype=np.float32))
    shared = _device_inputs(inputs)
    n = 8
    nseq = x.shape[0] // n
    nc, _ = build(nseq=nseq, nlayers=2)
    in_maps = []
    for i in range(n):
        m = dict(shared)
        m['x'] = np.ascontiguousarray(x[i * nseq:(i + 1) * nseq])
        in_maps.append(m)
    res = run_bass_kernel_spmd(nc, in_maps, core_ids=list(range(n)))
    return np.concatenate([np.asarray(r['out']) for r in res.results], axis=0).astype(np.float32)
```

```python
import math
import numpy as np
import concourse.bass as bass
import concourse.mybir as mybir
from concourse.bass_utils import run_bass_kernel_spmd
from contextlib import ExitStack

F32 = mybir.dt.float32
BF16 = mybir.dt.bfloat16
I32 = mybir.dt.int32
ALU = mybir.AluOpType
AF = mybir.ActivationFunctionType

ENGS = ['tensor', 'vector', 'scalar', 'gpsimd', 'sync']
SAME_ENGINE_SYNC = True
KB = 1024

L_SEQ = 2048
D = 1024
NH = 8
D_FF = 2816
EPS = 1e-6


class V:
    def __init__(self, ap, keys):
        self.ap = ap
        self.keys = keys

    def __getitem__(self, idx):
        return V(self.ap[idx], self.keys)

    def r(self, pat, **kw):
        return V(self.ap.rearrange(pat, **kw), self.keys)


def _keys(vs):
    out = []
    for v in vs:
        if isinstance(v, V):
            out.extend(v.keys)
        else:
            out.append(v)
    return out


class Sched:
    def __init__(self, nc, es):
        self.nc = nc
        self.es = es
        self.semh = {}
        self.cnt = {}
        for e in ENGS:
            self.semh[e] = es.enter_context(nc.semaphore('s_' + e))
            self.cnt[e] = 0
        self.prog = {e: [] for e in ENGS}
        self.seen = {e: {} for e in ENGS}
        self.st = {}
        self.nops = 0
        self.tag = ''
        self.tags = {e: [] for e in ENGS}

    def dma_sem(self, key):
        if key not in self.semh:
            self.semh[key] = self.es.enter_context(self.nc.semaphore('d_' + str(key)))
            self.cnt[key] = 0
        return key

    def _deps(self, eng, reads, writes):
        deps = {}
        st = self.st
        for b in reads:
            s = st.get(b)
            if s is not None and s[0] is not None:
                k, v = s[0]
                if deps.get(k, 0) < v:
                    deps[k] = v
        for b in writes:
            s = st.get(b)
            if s is not None:
                if s[0] is not None:
                    k, v = s[0]
                    if deps.get(k, 0) < v:
                        deps[k] = v
                for k, v in s[1].items():
                    if deps.get(k, 0) < v:
                        deps[k] = v
        seen = self.seen[eng]
        for k, v in deps.items():
            if k == eng and (eng in ('tensor', 'sync') or not SAME_ENGINE_SYNC):
                continue
            if seen.get(k, 0) < v:
                self.prog[eng].append(('wait', k, v))
                seen[k] = v

    def _record(self, ev, reads, writes):
        st = self.st
        k, v = ev
        for b in reads:
            s = st.get(b)
            if s is None:
                st[b] = [None, {k: v}]
            else:
                if s[1].get(k, 0) < v:
                    s[1][k] = v
        for b in writes:
            st[b] = [ev, {}]

    def op(self, eng, fn, reads=(), writes=(), inc=True):
        reads = _keys(reads)
        writes = _keys(writes)
        self._deps(eng, reads, writes)
        if eng != 'tensor':
            inc = True
        if inc:
            self.cnt[eng] += 1
            ev = (eng, self.cnt[eng])
        else:
            ev = (eng, self.cnt[eng] + 1)
        self._record(ev, reads, writes)
        self.prog[eng].append(('op', fn, inc))
        self.tags[eng].append(self.tag)
        self.nops += 1
        return ev

    def dma(self, queue, fns, sem, reads=(), writes=()):
        reads = _keys(reads)
        writes = _keys(writes)
        self.dma_sem(sem)
        self._deps(queue, reads, writes)
        if not isinstance(fns, (list, tuple)):
            fns = [fns]
        for fn in fns:
            self.cnt[sem] += 16
            self.prog[queue].append(('dma', fn, sem))
        ev = (sem, self.cnt[sem])
        self._record(ev, reads, writes)
        return ev

    def wait_all(self, eng, keys):
        self._deps(eng, _keys(keys), ())

    def finalize(self):
        nc = self.nc
        with nc.Block() as block:
            def mk(e):
                def body(eng):
                    for item in self.prog[e]:
                        if item[0] == 'wait':
                            eng.wait_ge(self.semh[item[1]], item[2])
                        elif item[0] == 'op':
                            r = item[1](eng)
                            if item[2]:
                                r.then_inc(self.semh[e], 1)
                        else:
                            item[1](eng).then_inc(self.semh[item[2]], 16)
                return body
            block.tensor(mk('tensor'))
            block.vector(mk('vector'))
            block.scalar(mk('scalar'))
            block.gpsimd(mk('gpsimd'))
            block.sync(mk('sync'))


class Mem:
    def __init__(self, nc, es, nbytes):
        self.t = es.enter_context(nc.sbuf_tensor("arena", [128, nbytes // 4], F32))
        self.nbytes = nbytes

    def view(self, off, n, dt, p0=0, p1=128):
        sz = 2 if dt == BF16 else 4
        nb = n * sz
        assert off % 4 == 0 and nb % 4 == 0 and off + nb <= self.nbytes, (off, nb)
        ap = self.t[p0:p1, off // 4:(off + nb) // 4]
        if dt != F32:
            ap = ap.bitcast(dt)
        keys = [(b, q) for b in range(off // KB, (off + nb - 1) // KB + 1)
                for q in range(p0 // 32, (p1 - 1) // 32 + 1)]
        return V(ap, keys)


FF_GROUPS = [(2560, 256), (0, 512), (512, 512), (1024, 512), (1536, 512), (2048, 512)]
_CONST_CACHE = {}


def _host_consts():
    if _CONST_CACHE:
        return _CONST_CACHE
    L = L_SEQ
    n = np.arange(1024, dtype=np.float64)
    CE = np.cos(2.0 * np.pi * np.outer(n, n) / 2048.0)
    SE = np.sin(2.0 * np.pi * np.outer(n, n) / 2048.0)
    CO = np.cos(2.0 * np.pi * np.outer(2 * n + 1, n) / 4096.0)
    SO = np.sin(2.0 * np.pi * np.outer(2 * n + 1, n) / 4096.0)

    def lay(T, w):
        g = 1024 // w
        return T.reshape(8, 128, g, w).transpose(2, 1, 0, 3)
    c = {}
    c['tfwd'] = np.ascontiguousarray(np.stack([lay(T, 128) for T in (CE, SE, CO, SO)], axis=2)).astype(np.float32).reshape(8, 128, 4096)
    c['tinv'] = np.ascontiguousarray(np.stack([lay(T, 256) for T in (CE, SE, CO.T, SO.T)], axis=2)).astype(np.float32).reshape(4, 128, 8192)
    f32 = np.float32
    t = np.linspace(0.0, 1.0, L, dtype=f32)[:, None]
    bands = 16
    freqs = np.linspace(1e-4, bands - 1, bands, dtype=f32)[None, :]
    w = (2.0 * math.pi * np.arange(L, dtype=f32)[:, None] / L).astype(f32)
    z = np.concatenate([t, np.cos(freqs * w), -np.sin(freqs * w)], axis=-1).astype(f32)
    c['zembT'] = np.ascontiguousarray(z.T)
    deltas = np.abs(np.linspace(math.log(1e-2) / 0.3, math.log(1e-2) / 1.5, 512, dtype=f32))
    c['decay'] = np.exp(-t * np.tile(deltas, 2)[None, :]).astype(f32)
    cst = np.zeros((128, 8), f32)
    inv = (1.0 / (10000.0 ** (np.arange(0, 32, 2, dtype=f32) / 32.0))).astype(f32)
    cst[64:80, 0] = inv / (2 * np.pi)
    cst[80:96, 0] = inv / (2 * np.pi)
    cst[64:80, 1] = -1.0
    cst[80:96, 1] = 1.0
    cst[:, 2] = 2.0 / 4096
    cst[0, 2] = 1.0 / 4096
    cst[:, 3] = 2.0 / 4096
    cst[:, 4] = EPS
    c['cst'] = cst
    cb = np.zeros((128, 258), f32)
    cb[:, 0:128] = np.eye(128)
    cb[:, 128:256] = 1.0
    cb[:, 256] = (-1.0) ** np.arange(128)
    c['cstb'] = cb
    c['altrow'] = ((-1.0) ** np.arange(256)).astype(f32)[None, :]
    _CONST_CACHE.update(c)
    return c


def _prep_weights(inp):
    f = lambda a: np.ascontiguousarray(np.asarray(a, dtype=np.float32))
    w = {}
    w_in = f(inp['w_in'])
    w['w_in'] = w_in
    ev = np.arange(0, 32, 2)
    od = ev + 1
    kpe = w_in[:, :, 384:416]
    kA = np.zeros((2, 1024, 96), np.float32)
    kB = np.zeros((2, 1024, 96), np.float32)
    kA[:, :, 64:80] = kpe[:, :, ev]
    kA[:, :, 80:96] = kpe[:, :, od]
    kB[:, :, 64:80] = kpe[:, :, od]
    kB[:, :, 80:96] = kpe[:, :, ev]
    w['wkpe'] = np.ascontiguousarray(np.stack([kA, kB], axis=2))
    wq = f(inp['w_q_up']).reshape(2, 256, 8, 96)
    qA = np.concatenate([wq[..., :64], wq[..., 64:][..., ev], wq[..., 64:][..., od]], axis=-1)
    qB = np.concatenate([wq[..., :64], wq[..., 64:][..., od], wq[..., 64:][..., ev]], axis=-1)
    w['wq'] = np.ascontiguousarray(np.stack([qA.reshape(2, 256, 768), qB.reshape(2, 256, 768)], axis=2))
    wkv = f(inp['w_kv_up']).reshape(2, 128, 8, 128)
    w['wk'] = np.ascontiguousarray(wkv[..., :64].reshape(2, 128, 512))
    w['wv'] = np.ascontiguousarray(wkv[..., 64:].reshape(2, 128, 512))
    wap = f(inp['w_attn_proj'])
    whp = f(inp['w_hy_proj'])
    mrg = np.zeros((2, 8, 128, 3072), np.float32)
    for l in range(2):
        for fc in range(8):
            ga = w_in[l][:, 1952 + fc * 128:1952 + (fc + 1) * 128].reshape(8, 128, 128).transpose(1, 0, 2).reshape(128, 1024)
            gb = w_in[l][:, 2976 + fc * 128:2976 + (fc + 1) * 128].reshape(8, 128, 128).transpose(1, 0, 2).reshape(128, 1024)
            ap = wap[l][:, fc * 128:(fc + 1) * 128].reshape(4, 128, 128).transpose(1, 0, 2).reshape(128, 512)
            hp = whp[l][:, fc * 128:(fc + 1) * 128].reshape(4, 128, 128).transpose(1, 0, 2).reshape(128, 512)
            mrg[l, fc] = np.concatenate([ga, gb, ap, hp], axis=1)
    w['mrg'] = mrg
    w['w_out'] = f(inp['w_out'])
    wg = f(inp['w_gate'])
    wu = f(inp['w_up'])
    wd = f(inp['w_down'])
    ffn = np.zeros((2, 6, 128, 12288), np.float32)
    for l in range(2):
        for gi, (g0, gs) in enumerate(FF_GROUPS):
            a = wg[l][:, g0:g0 + gs].reshape(8, 128, gs).transpose(1, 0, 2).reshape(128, 8 * gs)
            b = wu[l][:, g0:g0 + gs].reshape(8, 128, gs).transpose(1, 0, 2).reshape(128, 8 * gs)
            d = wd[l][g0:g0 + gs, :].reshape(gs // 128, 128, 1024).transpose(1, 0, 2).reshape(128, (gs // 128) * 1024)
            ffn[l, gi, :, 0:8 * gs] = a
            ffn[l, gi, :, 8 * gs:16 * gs] = b
            ffn[l, gi, :, 16 * gs:16 * gs + (gs // 128) * 1024] = d
    w['ffn'] = ffn
    w['norm_g'] = np.ascontiguousarray(np.concatenate(
        [f(inp['mix_norm_g']), f(inp['ffn_norm_g']), f(inp['final_norm_g'])[None, :]], axis=0))
    cols = np.zeros((2, 128, 64), np.float32)
    cw = f(inp['hy_conv_w'])
    cbias = f(inp['hy_conv_b'])
    for l in range(2):
        cols[l, :, 0:48] = np.stack([cw[l, 0], cw[l, 1], cw[l, 2], cbias[l]], axis=-1).reshape(12, 128, 4).transpose(1, 0, 2).reshape(128, 48)
        cols[l, :, 48:50] = f(inp['q_norm_g'])[l].reshape(2, 128).T
        cols[l, :, 50] = f(inp['kv_norm_g'])[l]
        for j, nm in enumerate(['filt_b1', 'filt_f1', 'filt_b2', 'filt_f2', 'filt_b3', 'filt_f3']):
            cols[l, 0:64, 52 + j] = f(inp[nm])[l]
    w['cols'] = cols
    w['hy_skip'] = f(inp['hy_skip'])
    w['filt_w1'] = f(inp['filt_w1'])
    w['filt_w2'] = f(inp['filt_w2'])
    w['filt_w3'] = f(inp['filt_w3'])
    w['filt_w4'] = f(inp['filt_w4'])
    w['positions'] = np.ascontiguousarray(np.asarray(inp['positions'], dtype=np.int32)).reshape(1, 2048)
    return w


X0, HT0, R10, ZY0, OT0, TB0, CS0, ARENA = 0, 65536, 98304, 131072, 147456, 163840, 196608, 210944


def build(nseq=4, nlayers=2, dbg=()):
    nc = bass.Bass("TRN2", target_bir_lowering=False)
    es = ExitStack()

    def din(name, shape, dt=F32):
        return nc.dram_tensor(name, list(shape), dt, kind="ExternalInput").ap()

    x_d = din("x", [nseq, 2048, 1024])
    pos_d = din("positions", [1, 2048], I32)
    w_in_d = din("w_in", [2, 1024, 4000])
    wkpe_d = din("wkpe", [2, 1024, 2, 96])
    wq_d = din("wq", [2, 256, 2, 768])
    wk_d = din("wk", [2, 128, 512])
    wv_d = din("wv", [2, 128, 512])
    mrg_d = din("mrg", [2, 8, 128, 3072])
    wout_d = din("w_out", [2, 1024, 1024])
    ffn_d = din("ffn", [2, 6, 128, 12288])
    ng_d = din("norm_g", [5, 1024])
    cols_d = din("cols", [2, 128, 64])
    skip_d = din("hy_skip", [2, 512])
    fw1_d = din("filt_w1", [2, 33, 64])
    fw2_d = din("filt_w2", [2, 64, 64])
    fw3_d = din("filt_w3", [2, 64, 64])
    fw4_d = din("filt_w4", [2, 64, 1024])
    tfwd_d = din("tfwd", [8, 128, 4 * 8 * 128])
    tinv_d = din("tinv", [4, 128, 4 * 8 * 256])
    zemb_d = din("zembT", [33, 2048])
    decay_d = din("decay", [2048, 1024])
    cst_d = din("cst", [128, 8])
    cstb_d = din("cstb", [128, 258])
    altrow_d = din("altrow", [1, 256])
    out_d = nc.dram_tensor("out", [nseq, 2048, 1024], F32, kind="ExternalOutput").ap()
    pspec_d = nc.dram_tensor("pspec", [2, 8, 128, 4 * 512], BF16).ap()
    mrgb_d = nc.dram_tensor("mrgb", [2, 8, 128, 3072], BF16).ap()
    tfwdb_d = nc.dram_tensor("tfwdb", [8, 128, 4 * 8 * 128], BF16).ap()
    tinvb_d = nc.dram_tensor("tinvb", [4, 128, 4 * 8 * 256], BF16).ap()
    dbg_out = {}

    S = Sched(nc, es)
    mem = Mem(nc, es, ARENA)
    pt = es.enter_context(nc.psum_tensor("psum", [128, 8, 512], F32))

    def PS(b, n=1):
        return V(pt[:, b:b + n, :] if n > 1 else pt[:, b, :], [('ps', b + i) for i in range(n)])

    def PSB(b):
        return V(pt[:, b, :].bitcast(BF16), [('ps', b)])

    class RR:
        def __init__(self, items):
            self.items = items
            self.i = 0

        def __call__(self):
            r = self.items[self.i % len(self.items)]
            self.i += 1
            return r

    def mm(out, lhsT, rhs, start, stop, inc=None):
        if inc is None:
            inc = stop
        S.op('tensor', lambda e: e.matmul(out.ap, lhsT=lhsT.ap, rhs=rhs.ap, start=start, stop=stop),
             reads=[lhsT, rhs], writes=[out], inc=inc)

    def tr(out, in_, ident):
        S.op('tensor', lambda e: e.transpose(out.ap, in_.ap, ident.ap), reads=[in_, ident], writes=[out])

    def vop(fn, reads, writes):
        S.op('vector', fn, reads=reads, writes=writes)

    def aop(fn, reads, writes):
        S.op('scalar', fn, reads=reads, writes=writes)

    def acopy(out, in_):
        aop(lambda e: e.activation(out=out.ap, in_=in_.ap, func=AF.Copy), [in_], [out])

    def vcopy(out, in_):
        vop(lambda e: e.tensor_copy(out=out.ap, in_=in_.ap), [in_], [out])

    cp_rr = RR(['v', 'a'])

    def anycopy(out, in_):
        if cp_rr() == 'v':
            vcopy(out, in_)
        else:
            acopy(out, in_)

    def vtt(out, a, b, op):
        vop(lambda e: e.tensor_tensor(out=out.ap, in0=a.ap, in1=b.ap, op=op), [a, b], [out])

    def wdma(out, src_ap, sem, queue='gpsimd'):
        S.dma(queue, lambda e: e.dma_start(out=out.ap, in_=src_ap), sem, writes=[out])

    def dump(name, v, shape, dt=F32):
        if name not in dbg or name in dbg_out:
            return
        d = nc.dram_tensor("dbg_" + name, list(shape), dt, kind="ExternalOutput").ap()
        dbg_out[name] = d
        S.dma('sync', lambda e: e.dma_start(out=d, in_=v.ap), 'dbg', reads=[v], writes=['DBGOUT'])

    cb = mem.view(CS0, 258, BF16)
    ident = cb[:, 0:128]
    ones = cb[:, 128:256]
    altc = cb[:, 256:257]
    cst = mem.view(CS0 + 516, 8, F32)
    epsc = cst[:, 4:5]
    colsv = [mem.view(CS0 + 548 + 256 * l, 64, F32) for l in range(2)]
    gB = mem.view(CS0 + 1060, 1024, F32)
    ROPE0 = CS0 + 5156
    cs_t = mem.view(ROPE0, 2048, BF16, 64, 96)
    sn_t = mem.view(ROPE0 + 4096, 2048, BF16, 64, 96)
    P1k = [[mem.view(ROPE0 + 2048 * l + 1024 * i, 512, BF16, 0, 1) for i in range(2)] for l in range(2)]
    G1k = [mem.view(ROPE0 + 4096 + 1024 * i, 512, BF16, 0, 1) for i in range(2)]
    altrow = mem.view(ROPE0 + 6144, 256, BF16, 0, 1)

    wdma(cb, cstb_d, 'c0')
    wdma(cst, cst_d, 'c1', 'sync')
    for l in range(2):
        wdma(colsv[l], cols_d[l], ('c2', l), 'sync')
    wdma(altrow, altrow_d, 'c3')

    Xt = [mem.view(X0 + t * 4096, 1024, F32) for t in range(16)]
    HTall = mem.view(HT0, 8 * 2048, BF16).r("p (k t) -> p k t", k=8)

    def HTs(k, t0, n):
        return mem.view(HT0 + k * 4096 + t0 * 2, n, BF16)

    def HT_tile(t):
        keys = []
        for k in range(8):
            keys += mem.view(HT0 + k * 4096 + t * 256, 128, BF16).keys
        return V(HTall.ap[:, :, t * 128:(t + 1) * 128], keys)

    def rope_tables():
        T0 = R10
        posi = mem.view(T0, 2048, I32, 64, 96)
        a = mem.view(T0 + 8192, 2048, F32, 64, 96)
        b = mem.view(T0 + 16384, 2048, F32, 64, 96)
        ki = mem.view(T0 + 24576, 2048, I32, 64, 96)
        S.dma('sync', lambda e: e.dma_start(out=posi.ap, in_=pos_d.partition_broadcast(32)), 'c4', writes=[posi])
        vcopy(a, posi)
        invc = cst[64:96, 0:1]
        sgnc = cst[64:96, 1:2]
        for which, off, dst in (('s', 0.5, sn_t), ('c', 0.75, cs_t)):
            vop(lambda e, off=off: e.tensor_scalar(out=b.ap, in0=a.ap, scalar1=invc.ap, scalar2=off, op0=ALU.mult, op1=ALU.add), [a, invc], [b])
            vcopy(ki, b)
            kf = mem.view(T0, 2048, F32, 64, 96)
            vop(lambda e: e.tensor_copy(out=kf.ap, in_=ki.ap), [ki], [kf])
            vtt(b, b, kf, ALU.subtract)
            vop(lambda e: e.tensor_single_scalar(out=kf.ap, in_=b.ap, scalar=0.0, op=ALU.is_lt), [b], [kf])
            vtt(b, b, kf, ALU.add)
            vop(lambda e: e.tensor_scalar(out=b.ap, in0=b.ap, scalar1=-0.5, scalar2=6.28318, op0=ALU.add, op1=ALU.mult), [b], [b])
            aop(lambda e: e.activation(out=kf.ap, in_=b.ap, func=AF.Sin), [b], [kf])
            if which == 's':
                vop(lambda e, dst=dst: e.tensor_scalar(out=dst.ap, in0=kf.ap, scalar1=sgnc.ap, scalar2=None, op0=ALU.mult), [kf, sgnc], [dst])
            else:
                vcopy(dst, kf)

    rope_tables()
    dump('cs', cs_t, [32, 2048], BF16)
    dump('sn', sn_t, [32, 2048], BF16)

    def range_sin(dst, arg, tmp_i, tmp_f, np_, n):
        vop(lambda e: e.tensor_scalar(out=arg.ap, in0=arg.ap, scalar1=1.0 / (2 * math.pi), scalar2=16.5, op0=ALU.mult, op1=ALU.add), [arg], [arg])
        vcopy(tmp_i, arg)
        vcopy(tmp_f, tmp_i)
        vtt(arg, arg, tmp_f, ALU.subtract)
        vop(lambda e: e.tensor_single_scalar(out=tmp_f.ap, in_=arg.ap, scalar=0.0, op=ALU.is_lt), [arg], [tmp_f])
        vtt(arg, arg, tmp_f, ALU.add)
        vop(lambda e: e.tensor_scalar(out=arg.ap, in0=arg.ap, scalar1=-0.5, scalar2=6.28318, op0=ALU.add, op1=ALU.mult), [arg], [arg])
        aop(lambda e: e.activation(out=dst.ap, in_=arg.ap, func=AF.Sin), [arg], [dst])

    def filter_prologue(preload=None):
        S.tag = 'prologue'
        o = X0
        zemb = mem.view(o, 2048, F32, 0, 33); o += 8192
        hA = mem.view(o, 2048, F32, 0, 64); o += 8192
        hB = mem.view(o, 2048, F32, 0, 64); o += 8192
        arg = mem.view(o, 2048, F32, 0, 64); o += 8192
        ti = mem.view(o, 2048, I32, 0, 64); o += 8192
        tf = mem.view(o, 2048, F32, 0, 64); o += 8192
        dec = [mem.view(o + 4096 * i, 1024, F32) for i in range(2)]; o += 8192
        filt = mem.view(o, 1024, F32); o += 4096
        assert o <= HT0
        o = HT0
        tfs = [mem.view(o + 8192 * i, 4096, BF16).r("p (q a i) -> p q a i", q=4, a=8) for i in range(2)]; o += 16384
        stg = [mem.view(o + 8192 * i, 2048, BF16).r("p (q c) -> p q c", q=4) for i in range(2)]; o += 16384
        ocs = mem.view(o, 512, F32); o += 2048
        oss = mem.view(o, 512, F32); o += 2048
        bt = [mem.view(o + 2048 * i, 512, F32) for i in range(2)]; o += 4096
        PL = []
        for l in range(nlayers):
            d = {}
            d['w1'] = mem.view(o, 64, F32, 0, 33); o += 256
            d['w2'] = mem.view(o, 64, F32, 0, 64); o += 256
            d['w3'] = mem.view(o, 64, F32, 0, 64); o += 256
            d['w4'] = mem.view(o, 1024, F32, 0, 64); o += 4096
            d['skipB'] = mem.view(o, 512, F32); o += 2048
            d['skipA'] = [mem.view(o + 2048 * i, 512, F32) for i in range(2)]; o += 4096
            d['fpo'] = o; o += 16384
            d['fmo'] = o; o += 16384
            PL.append(d)
        assert o <= CS0, o
        S.dma('sync', lambda e: e.dma_start(out=zemb.ap, in_=zemb_d), 'f0', writes=[zemb])

        def fpa(l, r, a):
            return mem.view(PL[l]['fpo'] + r * 8192 + a * 1024, 512, BF16)

        def fma(l, r, a):
            return mem.view(PL[l]['fmo'] + r * 8192 + a * 1024, 512, BF16)

        for l in range(nlayers):
            d = PL[l]
            w1, w2, w3, w4, skipB, skipA = d['w1'], d['w2'], d['w3'], d['w4'], d['skipB'], d['skipA']
            S.dma('sync', [lambda e, w1=w1, l=l: e.dma_start(out=w1.ap, in_=fw1_d[l]),
                           lambda e, w2=w2, l=l: e.dma_start(out=w2.ap, in_=fw2_d[l]),
                           lambda e, w3=w3, l=l: e.dma_start(out=w3.ap, in_=fw3_d[l]),
                           lambda e, w4=w4, l=l: e.dma_start(out=w4.ap, in_=fw4_d[l]),
                           lambda e, skipB=skipB, l=l: e.dma_start(out=skipB.ap, in_=skip_d[l:l + 1, :].partition_broadcast(128))],
                  ('f1', l), writes=[w1, w2, w3, w4, skipB])
            cl = colsv[l]
            srcs = [(w1, zemb), (w2, hA), (w3, hB)]
            dsts = [hA, hB, hA]
            for i in range(3):
                wgt, src = srcs[i]
                pb = 4 * (i % 2)
                for tcn in range(4):
                    mm(PS(pb + tcn)[0:64, :], wgt, src[:, tcn * 512:(tcn + 1) * 512], True, True)
                bcol = cl[0:64, 52 + 2 * i:53 + 2 * i]
                fcol = cl[0:64, 53 + 2 * i:54 + 2 * i]
                pin = V(pt[0:64, pb:pb + 4, :], [('ps', pb + j) for j in range(4)])
                vop(lambda e, pin=pin, bcol=bcol, fcol=fcol: e.tensor_scalar(
                    out=arg.ap.rearrange("p (a c) -> p a c", a=4), in0=pin.ap, scalar1=bcol.ap, scalar2=fcol.ap,
                    op0=ALU.add, op1=ALU.mult), [pin, bcol, fcol], [arg])
                range_sin(dsts[i], arg, ti, tf, 64, 2048)
            h3 = hA
            cnt = 0
            for r in range(2):
                for a in range(8):
                    dv = dec[cnt % 2]
                    S.dma('sync', lambda e, dv=dv, a=a, r=r: e.dma_start(out=dv.ap, in_=decay_d[256 * a + r:256 * a + 256:2, :]), ('dec', cnt % 2), writes=[dv])
                    pb = 2 * (cnt % 2)
                    cnt += 1
                    hv = V(h3.ap[:, 256 * a + r:256 * a + 256:2], h3.keys)
                    for hlf in range(2):
                        mm(PS(pb + hlf), hv, w4[:, hlf * 512:(hlf + 1) * 512], True, True)
                    pin = PS(pb, 2)
                    vop(lambda e, pin=pin, dv=dv: e.tensor_tensor(out=filt.ap.rearrange("p (a c) -> p a c", a=2), in0=pin.ap,
                                                                 in1=dv.ap.rearrange("p (a c) -> p a c", a=2), op=ALU.mult), [pin, dv], [filt])
                    if a == 0 and r == 0:
                        vop(lambda e: e.memset(filt.ap[0:1, 512:1024], 0.0), [], [filt])
                    vtt(fpa(l, r, a), filt[:, 0:512], filt[:, 512:1024], ALU.add)
                    vtt(fma(l, r, a), filt[:, 0:512], filt[:, 512:1024], ALU.subtract)
            for i in range(2):
                ac = cst[:, 2 + i:3 + i]
                vop(lambda e, i=i, ac=ac, skipA=skipA, skipB=skipB: e.tensor_scalar(out=skipA[i].ap, in0=skipB.ap, scalar1=ac.ap, scalar2=None, op0=ALU.mult), [skipB, ac], [skipA[i]])
            pn = PS(6)[0:1, :]
            for a in range(8):
                mm(pn, altc, fpa(l, 0, a), a == 0, a == 7)
            pn2 = PS(7)[0:1, :]
            for a in range(8):
                mm(pn2, altc, fma(l, 1, a), a == 0, a == 7)
            vtt(bt[0][0:1, :], pn, skipB[0:1, :], ALU.add)
            vop(lambda e, l=l: e.tensor_scalar(out=P1k[l][0].ap, in0=bt[0].ap[0:1, :], scalar1=2.0 / 4096, scalar2=None, op0=ALU.mult), [bt[0]], [P1k[l][0]])
            vop(lambda e, l=l, pn2=pn2: e.tensor_scalar(out=P1k[l][1].ap, in0=pn2.ap, scalar1=2.0 / 4096, scalar2=None, op0=ALU.mult), [pn2], [P1k[l][1]])
        if preload is not None:
            preload()
        cnt = 0
        for j in range(8):
            sl = j % 2
            S.dma('gpsimd', lambda e, sl=sl, j=j: e.dma_start(out=tfs[sl].ap.rearrange("p q a i -> p (q a i)"), in_=tfwd_d[j]), ('tfwd', sl), writes=[tfs[sl]])
            for l in range(nlayers):
                skipA = PL[l]['skipA']
                b0 = 4 * (cnt % 2)
                sg_ = stg[cnt % 2]
                cnt += 1
                pec, pes, poc, pos_ = PS(b0), PS(b0 + 1), PS(b0 + 2), PS(b0 + 3)
                for (pp, q, fn, r) in ((pec, 0, fpa, 0), (pes, 1, fma, 0), (poc, 2, fpa, 1), (pos_, 3, fma, 1)):
                    for a in range(8):
                        mm(pp, tfs[sl][:, q, a, :], fn(l, r, a), a == 0, a == 7)
                acopy(ocs, poc)
                acopy(oss, pos_)
                ai = 0 if j == 0 else 1
                ac = cst[:, 2 + ai:3 + ai]
                sk = skipA[ai]
                vtt(bt[0], pec, ocs, ALU.add)
                vop(lambda e, ac=ac, sk=sk, sg_=sg_: e.scalar_tensor_tensor(out=sg_.ap[:, 0, :], in0=bt[0].ap, scalar=ac.ap, in1=sk.ap, op0=ALU.mult, op1=ALU.add), [bt[0], ac, sk], [sg_])
                vtt(bt[1], pes, oss, ALU.add)
                vop(lambda e, ac=ac, sg_=sg_: e.tensor_scalar(out=sg_.ap[:, 1, :], in0=bt[1].ap, scalar1=ac.ap, scalar2=None, op0=ALU.mult), [bt[1], ac], [sg_])
                vtt(bt[0], pec, ocs, ALU.subtract)
                vop(lambda e, ac=ac, sk=sk, sg_=sg_: e.scalar_tensor_tensor(out=sg_.ap[:, 2, :], in0=bt[0].ap, scalar=ac.ap, in1=sk.ap, op0=ALU.mult, op1=ALU.add), [bt[0], ac, sk], [sg_])
                vtt(bt[1], oss, pes, ALU.subtract)
                vop(lambda e, ac=ac, sg_=sg_: e.tensor_scalar(out=sg_.ap[:, 3, :], in0=bt[1].ap, scalar1=ac.ap, scalar2=None, op0=ALU.mult), [bt[1], ac], [sg_])
                S.dma('sync', lambda e, sg_=sg_, j=j, l=l: e.dma_start(out=pspec_d[l, j], in_=sg_.ap.rearrange("p q c -> p (q c)")),
                      ('pspec_w', (cnt - 1) % 2), reads=[sg_], writes=[('pspec', l, j)])

    psr = RR([0, 1, 2, 3, 4, 5, 6, 7])
    SCALE = 96.0 ** -0.5

    def load_x(b, tiles=range(16)):
        for t in tiles:
            S.dma('sync', lambda e, t=t: e.dma_start(out=Xt[t].ap, in_=x_d[b, t * 128:(t + 1) * 128, :]), ('x', t), writes=[Xt[t]])

    def rms_stats_begin(tmp0):
        blk = mem.view(tmp0 + 2048, 256, F32)
        vop(lambda e: e.memset(blk.ap, 0.0), [], [blk])
        jk = mem.view(tmp0, 1024, BF16)
        vop(lambda e: e.memset(jk.ap[:, 0:2], 0.0), [], [jk])
        return mem.view(tmp0 + 2048, 64, F32), blk, jk

    def rms_stats_act(ctx):
        stv, blk, jk = ctx
        for g in range(4):
            for t in range(4 * g, 4 * g + 4):
                S.op('scalar', lambda e, t=t: e.activation(out=jk.ap, in_=Xt[t].ap, func=AF.Square, accum_out=stv.ap[:, t:t + 1]),
                     reads=[Xt[t], blk], writes=[('ss', t), jk])
            ssk = [('ss', t) for t in range(4 * g, 4 * g + 4)]
            S.op('scalar', lambda e, g=g: e.activation(out=stv.ap[:, 32 + 4 * g:36 + 4 * g], in_=stv.ap[:, 4 * g:4 * g + 4], func=AF.Sqrt, scale=1.0 / 1024, bias=epsc.ap),
                 reads=ssk + [blk, epsc], writes=[('sd', g)])

    def rms_rstd(ctx, g):
        stv, blk, jk = ctx
        S.op('vector', lambda e: e.reciprocal(out=stv.ap[:, 48 + 4 * g:52 + 4 * g], in_=stv.ap[:, 32 + 4 * g:36 + 4 * g]), reads=[('sd', g), blk], writes=[('rstd', g)])

    gB2 = mem.view(TB0 + 20480, 1024, F32)

    def load_gain(gidx, dst=None):
        dst = gB if dst is None else dst
        S.dma('sync', lambda e: e.dma_start(out=dst.ap, in_=ng_d[gidx:gidx + 1, :].partition_broadcast(128)), ('gB', 0 if dst is gB else 1), writes=[dst])

    def norm_to_hT(gB, tmp0):
        S.tag = 'norm'
        hn = [mem.view(tmp0 + 4096 + 2048 * j, 1024, BF16) for j in range(4)]
        ctx = rms_stats_begin(tmp0)
        stv, blk, _ = ctx
        rms_stats_act(ctx)
        for t in range(16):
            j = t % 4
            if t % 4 == 0:
                rms_rstd(ctx, t // 4)
            S.op('vector', lambda e, t=t, j=j: e.scalar_tensor_tensor(out=hn[j].ap, in0=Xt[t].ap, scalar=stv.ap[:, 48 + t:49 + t], in1=gB.ap, op0=ALU.mult, op1=ALU.mult),
                 reads=[Xt[t], ('rstd', t // 4), blk, gB], writes=[hn[j]])
            pb = psr()
            pv = PSB(pb)
            for k in range(8):
                tr(pv[:, k * 128:(k + 1) * 128], hn[j][:, k * 128:(k + 1) * 128], ident)
            if t < 8 or t % 2 == 0:
                vcopy(HT_tile(t), pv.r("p (k t) -> p k t", k=8))
            else:
                acopy(HT_tile(t), pv.r("p (k t) -> p k t", k=8))

    NT0 = TB0 + 24576

    def hn_slot(t):
        j = t % 4
        return mem.view(NT0 + 4096 + 2048 * j, 1024, BF16) if j < 2 else mem.view(TB0 + 16384 + 2048 * (j - 2), 1024, BF16)

    def norm_group_a(ctx, g, gBv, mode, b=None):
        tg = S.tag
        S.tag = 'norm_i'
        stv, blk, jk = ctx
        for t in range(4 * g, 4 * g + 4):
            S.op('scalar', lambda e, t=t: e.activation(out=jk.ap, in_=Xt[t].ap, func=AF.Square, accum_out=stv.ap[:, t:t + 1]),
                 reads=[Xt[t], blk], writes=[('ss', t), jk])
        ssk = [('ss', t) for t in range(4 * g, 4 * g + 4)]
        S.op('scalar', lambda e, g=g: e.activation(out=stv.ap[:, 32 + 4 * g:36 + 4 * g], in_=stv.ap[:, 4 * g:4 * g + 4], func=AF.Sqrt, scale=1.0 / 1024, bias=epsc.ap),
             reads=ssk + [blk, epsc], writes=[('sd', g)])
        rms_rstd(ctx, g)
        for t in range(4 * g, 4 * g + 4):
            if mode == 'hT':
                hn = hn_slot(t)
                S.op('vector', lambda e, t=t, hn=hn: e.scalar_tensor_tensor(out=hn.ap, in0=Xt[t].ap, scalar=stv.ap[:, 48 + t:49 + t], in1=gBv.ap, op0=ALU.mult, op1=ALU.mult),
                     reads=[Xt[t], ('rstd', g), blk, gBv], writes=[hn])
            else:
                stg = mem.view(R10 + 20480 + 4096 * (t % 3), 1024, F32)
                S.op('vector', lambda e, t=t, stg=stg: e.scalar_tensor_tensor(out=stg.ap, in0=Xt[t].ap, scalar=stv.ap[:, 48 + t:49 + t], in1=gBv.ap, op0=ALU.mult, op1=ALU.mult),
                     reads=[Xt[t], ('rstd', g), blk, gBv], writes=[stg])
                S.dma('sync', lambda e, t=t, stg=stg: e.dma_start(out=out_d[b, t * 128:(t + 1) * 128, :], in_=stg.ap), ('out', t % 3), reads=[stg], writes=[('OUT', b, t)])
        S.tag = tg

    def norm_group_b(g, pbank=7):
        tg = S.tag
        S.tag = 'norm_i'
        for t in range(4 * g, 4 * g + 4):
            hn = hn_slot(t)
            pv = PSB(pbank)
            for k in range(8):
                tr(pv[:, k * 128:(k + 1) * 128], hn[:, k * 128:(k + 1) * 128], ident)
            if t % 2 == 0:
                vcopy(HT_tile(t), pv.r("p (k t) -> p k t", k=8))
            else:
                acopy(HT_tile(t), pv.r("p (k t) -> p k t", k=8))
        S.tag = tg

    def hyena(b, l):
        S.tag = 'hy_conv'
        cl = colsv[l]
        us = [mem.view(R10 + 9216 * i, 2052, F32) for i in range(2)]
        t1s = [mem.view(R10 + 18432, 2048, F32), mem.view(TB0 + 12288, 2048, F32)]
        zT = mem.view(R10 + 26624, 2048, BF16)
        x1c = mem.view(TB0 + 4096, 2048, F32)
        whs = [mem.view(TB0 + 2048 * i, 1024, BF16).r("p (k c) -> p k c", k=8) for i in range(2)]
        ztok = mem.view(ZY0, 16 * 512, BF16).r("p (r a c) -> p r a c", r=2, a=8)
        x0T = [mem.view(OT0 + 4096 * c, 2048, BF16) for c in range(4)]
        for u in us:
            vop(lambda e, u=u: e.memset(u.ap[:, 0:1], 0.0), [], [u])
            vop(lambda e, u=u: e.memset(u.ap[:, 2049:2050], 0.0), [], [u])
        wcnt = [0]
        cbank = RR([0, 1, 2, 3, 4, 5])
        tbank = RR([6, 7])

        def conv(j, dst):
            sl = wcnt[0] % 2
            wcnt[0] += 1
            u = us[sl]
            t1 = t1s[sl]
            if dst is None:
                dst = t1
            wdma(whs[sl], w_in_d[l, :, 416 + 128 * j:416 + 128 * (j + 1)].rearrange("(k p) c -> p k c", p=128), ('whs', sl))
            w0, w1, w2, bb = [cl[:, 4 * j + i:4 * j + i + 1] for i in range(4)]
            for tcn in range(4):
                pin = PS(cbank())
                for k in range(8):
                    mm(pin, whs[sl][:, k, :], HTs(k, tcn * 512, 512), k == 0, k == 7)
                uv = V(u.ap[:, 1 + tcn * 512:1 + (tcn + 1) * 512], mem.view(R10 + 9216 * sl + 4 + tcn * 2048, 512, F32).keys)
                tv = V(t1.ap[:, tcn * 512:(tcn + 1) * 512], t1.keys)
                aop(lambda e, uv=uv, pin=pin: e.activation(out=uv.ap, in_=pin.ap, func=AF.Copy), [pin], [uv])
                aop(lambda e, tv=tv, pin=pin: e.activation(out=tv.ap, in_=pin.ap, func=AF.Identity, scale=w1.ap, bias=bb.ap), [pin, w1, bb], [tv])
            vop(lambda e: e.scalar_tensor_tensor(out=t1.ap, in0=u.ap[:, 0:2048], scalar=w0.ap, in1=t1.ap, op0=ALU.mult, op1=ALU.add), [u, w0, t1], [t1])
            vop(lambda e: e.scalar_tensor_tensor(out=dst.ap, in0=u.ap[:, 2:2050], scalar=w2.ap, in1=t1.ap, op0=ALU.mult, op1=ALU.add), [u, w2, t1], [dst])
            return dst

        def ztrans(cc):
            for r in range(2):
                pv = PSB(tbank())
                for a in range(8):
                    tr(pv[:, a * 128:(a + 1) * 128], V(zT.ap[:, 256 * a + r:256 * a + 256:2], zT.keys), ident)
                keys = []
                for a in range(8):
                    keys += mem.view(ZY0 + r * 8192 + a * 1024 + cc * 256, 128, BF16).keys
                dstv = V(ztok.ap[:, r, :, cc * 128:(cc + 1) * 128], keys)
                anycopy(dstv, pv.r("p (a c) -> p a c", a=8))

        for cc in range(4):
            conv(4 + cc, x1c)
            if cc > 0:
                ztrans(cc - 1)
            vc = conv(8 + cc, None)
            vtt(zT, vc, x1c, ALU.mult)
        ztrans(3)
        dump('ztok', V(ztok.ap, mem.view(ZY0, 8192, BF16).keys), [128, 2, 8, 512], BF16)

        S.tag = 'hy_fwd'
        AB = [mem.view(R10 + 8192 * i, 4096, BF16).r("p (a c) -> p a c", a=8) for i in range(4)]

        def zt(r, a):
            return mem.view(ZY0 + r * 8192 + a * 1024, 512, BF16)

        def abv(i, a, c0=0, n=512):
            return V(AB[i].ap[:, a, c0:c0 + n], mem.view(R10 + 8192 * i + a * 1024 + c0 * 2, n, BF16).keys)

        tfs = [mem.view(TB0 + 8192 * i, 4096, BF16).r("p (q a i) -> p q a i", q=4, a=8) for i in range(2)]
        Pt = [mem.view(TB0 + 16384 + 8192 * i, 2048, BF16).r("p (q c) -> p q c", q=4) for i in range(2)]
        tm = [mem.view(OT0 + 2048 * i, 512, BF16) for i in range(8)]
        for j in range(8):
            sl = j % 2
            S.dma('sync', lambda e, sl=sl, j=j: e.dma_start(out=tfs[sl].ap.rearrange("p q a i -> p (q a i)"), in_=tfwdb_d[j]), ('tfwdh', sl), reads=[('tfb', j)], writes=[tfs[sl]])
            S.dma('sync', lambda e, sl=sl, j=j: e.dma_start(out=Pt[sl].ap.rearrange("p q c -> p (q c)"), in_=pspec_d[l, j]),
                  ('pspec_r', sl), reads=[('pspec', l, j)], writes=[Pt[sl]])
            b0 = 4 * sl
            pec, pes, poc, pos_ = PS(b0), PS(b0 + 1), PS(b0 + 2), PS(b0 + 3)
            for (pp, q, r) in ((pec, 0, 0), (pes, 1, 0), (poc, 2, 1), (pos_, 3, 1)):
                for a in range(8):
                    mm(pp, tfs[sl][:, q, a, :], zt(r, a), a == 0, a == 7)
            ocs, oss, xc, xcm, xs, t2, ecs, ess = tm
            pr, qs, prm, qsm = [Pt[sl][:, q, :] for q in range(4)]
            acopy(ecs, pec)
            acopy(ocs, poc)
            acopy(ess, pes)
            acopy(oss, pos_)
            vtt(xc, ecs, ocs, ALU.add)
            vtt(xcm, ecs, ocs, ALU.subtract)
            vtt(xs, ess, oss, ALU.add)
            vtt(oss, oss, ess, ALU.subtract)
            xsm = oss
            vtt(ocs, xc, pr, ALU.mult)
            vtt(t2, xs, qs, ALU.mult)
            vtt(ocs, ocs, t2, ALU.subtract)
            vtt(xs, xs, pr, ALU.mult)
            vtt(xc, xc, qs, ALU.mult)
            vtt(xs, xs, xc, ALU.add)
            vtt(xc, xcm, prm, ALU.mult)
            vtt(t2, xsm, qsm, ALU.mult)
            vtt(xc, xc, t2, ALU.subtract)
            vtt(xsm, xsm, prm, ALU.mult)
            vtt(xcm, xcm, qsm, ALU.mult)
            vtt(xsm, xsm, xcm, ALU.add)
            vtt(abv(0, j), ocs, xc, ALU.add)
            vtt(abv(2, j), ocs, xc, ALU.subtract)
            vtt(abv(1, j), xs, xsm, ALU.subtract)
            vtt(abv(3, j), xs, xsm, ALU.add)
        px, px2 = PS(0)[0:1, :], PS(1)[0:1, :]
        for a in range(8):
            mm(px, altc, zt(0, a), a == 0, a == 7)
        for a in range(8):
            mm(px2, altc, zt(1, a), a == 0, a == 7)
        k0, k1, k2 = tm[0][0:1, :], tm[1][0:1, :], tm[2][0:1, :]
        vtt(k0, px, P1k[l][0], ALU.mult)
        vtt(k1, px2, P1k[l][1], ALU.mult)
        vtt(G1k[0], k0, k1, ALU.subtract)
        vtt(k0, px2, P1k[l][0], ALU.mult)
        vtt(k1, px, P1k[l][1], ALU.mult)
        vtt(G1k[1], k0, k1, ALU.add)

        S.tag = 'hy_inv'
        yT = [mem.view(ZY0 + 4096 * c, 2048, BF16) for c in range(4)]
        tis = [mem.view(TB0 + 16384 * i, 8192, BF16).r("p (q a i) -> p q a i", q=4, a=8) for i in range(2)]
        for g in range(4):
            sl = g % 2
            S.dma('sync', lambda e, sl=sl, g=g: e.dma_start(out=tis[sl].ap.rearrange("p q a i -> p (q a i)"), in_=tinvb_d[g]), ('tinvh', sl), reads=[('tib', g)], writes=[tis[sl]])
            for cc in range(4):
                for r in range(2):
                    po = PS(psr())[:, 0:256]
                    for a in range(8):
                        mm(po, abv(2 * r, a, cc * 128, 128), tis[sl][:, 2 * r, a, :], a == 0, False, inc=False)
                    for a in range(8):
                        mm(po, abv(2 * r + 1, a, cc * 128, 128), tis[sl][:, 2 * r + 1, a, :], False, False, inc=False)
                    mm(po, G1k[r][:, cc * 128:(cc + 1) * 128], altrow, False, True)
                    yv = V(yT[cc].ap[:, g * 512 + r:(g + 1) * 512:2], mem.view(ZY0 + 4096 * cc + g * 1024, 512, BF16).keys)
                    acopy(yv, po)
        S.tag = 'hy_conv'
        for cc in range(4):
            conv(cc, x0T[cc])
            vtt(yT[cc], yT[cc], x0T[cc], ALU.mult)
        dump('yT', mem.view(ZY0, 8192, BF16), [128, 8192], BF16)

    def attention(b, l):
        S.tag = 'latents'
        cl = colsv[l]
        qlatT = [mem.view(TB0 + 4096 * k, 2048, BF16) for k in range(2)]
        kvlatT = mem.view(TB0 + 8192, 2048, BF16)
        kpeT = mem.view(TB0 + 12288, 2048, BF16, 64, 96)
        wqv = mem.view(TB0 + 16384, 2 * 2 * 768, BF16).r("p (k v c) -> p k v c", k=2, v=2)
        wkv_ = mem.view(TB0 + 22528, 512, BF16)
        wvv = mem.view(TB0 + 23552, 512, BF16)
        sq = [mem.view(TB0 + 24576, 3 * 512, BF16).r("p (a c) -> p a c", a=3), mem.view(R10 + 20480, 3 * 512, BF16).r("p (a c) -> p a c", a=3)]
        rq = [mem.view(TB0 + 27648 + 2048 * i, 512, F32) for i in range(2)]
        wlat = mem.view(R10, 8 * 384, BF16).r("p (k c) -> p k c", k=8)
        wkp = mem.view(R10 + 6144, 8 * 192, BF16).r("p (k v c) -> p k v c", k=8, v=2)
        tmpA = mem.view(R10 + 12288, 512, F32)
        tmpB = mem.view(R10 + 14336, 512, F32)
        tmpA2 = mem.view(R10 + 16384, 512, F32)
        tmpB2 = mem.view(R10 + 18432, 512, F32)
        wdma(wlat, w_in_d[l, :, 0:384].rearrange("(k p) c -> p k c", p=128), 'wlat')
        wdma(wkp, wkpe_d[l].rearrange("(k p) v c -> p k v c", p=128), 'wkp')
        wdma(wqv, wq_d[l].rearrange("(k p) v c -> p k v c", p=128), 'wq')
        wdma(wkv_, wk_d[l], 'wk')
        wdma(wvv, wv_d[l], 'wv')
        gq = [cl[:, 48 + k:49 + k] for k in range(2)]
        gk = cl[:, 50:51]
        for tcn in range(4):
            tsl = slice(tcn * 512, (tcn + 1) * 512)
            pa, pbb = PS(2 * (tcn % 2)), PS(2 * (tcn % 2) + 1)
            for k in range(8):
                mm(pa[0:96, :], wkp[:, k, 0, :], HTs(k, tcn * 512, 512), k == 0, k == 7)
            for k in range(8):
                mm(pbb[0:96, :], wkp[:, k, 1, :], HTs(k, tcn * 512, 512), k == 0, k == 7)
            csv = V(cs_t.ap[:, tsl], cs_t.keys)
            snv = V(sn_t.ap[:, tsl], sn_t.keys)
            tA = tmpA if tcn % 2 == 0 else tmpA2
            tB = tmpB if tcn % 2 == 0 else tmpB2
            vtt(tA[64:96, :], pa[64:96, :], csv, ALU.mult)
            vtt(tB[64:96, :], pbb[64:96, :], snv, ALU.mult)
            dst = V(kpeT.ap[:, tsl], mem.view(TB0 + 12288 + tcn * 1024, 512, BF16, 64, 96).keys)
            vtt(dst, tA[64:96, :], tB[64:96, :], ALU.add)
        for tcn in range(4):
            tsl = slice(tcn * 512, (tcn + 1) * 512)
            b0 = 3 * (tcn % 2)
            pq = [PS(b0), PS(b0 + 1)]
            pk = PS(b0 + 2)
            for k in range(8):
                mm(pq[0], wlat[:, k, 0:128], HTs(k, tcn * 512, 512), k == 0, k == 7)
            for k in range(8):
                mm(pq[1], wlat[:, k, 128:256], HTs(k, tcn * 512, 512), k == 0, k == 7)
            for k in range(8):
                mm(pk, wlat[:, k, 256:384], HTs(k, tcn * 512, 512), k == 0, k == 7)
            sqv = sq[tcn % 2]
            for i, p in enumerate([pq[0], pq[1], pk]):
                aop(lambda e, i=i, p=p, sqv=sqv: e.activation(out=sqv.ap[:, i, :], in_=p.ap, func=AF.Square), [p], [sqv])
            pss, psk = PS(6), PS(7)
            mm(pss, ones, sqv[:, 0, :], True, False, inc=False)
            mm(pss, ones, sqv[:, 1, :], False, True)
            mm(psk, ones, sqv[:, 2, :], True, True)
            for (p, n, r) in ((pss, 256, rq[0]), (psk, 128, rq[1])):
                aop(lambda e, p=p, n=n, r=r: e.activation(out=r.ap, in_=p.ap, func=AF.Ln, scale=1.0 / n, bias=epsc.ap), [p, epsc], [r])
                aop(lambda e, r=r: e.activation(out=r.ap, in_=r.ap, func=AF.Exp, scale=-0.5), [r], [r])
            for k in range(2):
                dst = V(qlatT[k].ap[:, tsl], mem.view(TB0 + 4096 * k + tcn * 1024, 512, BF16).keys)
                vop(lambda e, k=k, dst=dst, pq=pq: e.scalar_tensor_tensor(out=dst.ap, in0=pq[k].ap, scalar=gq[k].ap, in1=rq[0].ap, op0=ALU.mult, op1=ALU.mult),
                    [pq[k], gq[k], rq[0]], [dst])
            dst = V(kvlatT.ap[:, tsl], mem.view(TB0 + 8192 + tcn * 1024, 512, BF16).keys)
            vop(lambda e, dst=dst, pk=pk: e.scalar_tensor_tensor(out=dst.ap, in0=pk.ap, scalar=gk.ap, in1=rq[1].ap, op0=ALU.mult, op1=ALU.mult),
                [pk, gk, rq[1]], [dst])
        dump('qlatT', mem.view(TB0, 4096, BF16), [128, 4096], BF16)
        dump('kvlatT', kvlatT, [128, 2048], BF16)
        dump('kpeT', kpeT, [32, 2048], BF16)

        QT = [mem.view(R10 + 4096 * i, 2048, BF16) for i in range(2)]
        KT = [mem.view(R10 + 8192 + 4096 * i, 2048, BF16) for i in range(2)]
        VA = [mem.view(R10 + 16384 + 4096 * i, 2048, BF16).r("p (a c) -> p a c", a=16) for i in range(2)]
        PT = [mem.view(R10 + 24576 + 2048 * i, 1024, BF16).r("p (a c) -> p a c", a=2) for i in range(3)]
        tmpA = mem.view(TB0 + 24576, 512, F32)
        tmpB = mem.view(R10 + 30720, 512, F32)
        OT = [mem.view(OT0 + 4096 * j, 2048, BF16) for j in range(4)]
        rden = [mem.view(TB0 + 27648 + 2048 * i, 512, F32) for i in range(2)]
        vop(lambda e: e.memset(VA[0].ap[:, :, 64:128], 1.0), [], [VA[0]])
        vop(lambda e: e.memset(VA[1].ap[:, :, 0:64], 1.0), [], [VA[1]])
        ptr = RR([0, 1, 2])
        s2r = RR([0, 2])
        orr = RR([4, 5])
        def qkv_pieces(h):
            sl = h % 2
            pieces = []

            def q_piece(tcn):
                S.tag = 'qkv'
                tsl = slice(tcn * 512, (tcn + 1) * 512)
                pa, pbb = PS(6), PS(7)
                qv = [V(qlatT[k].ap[:, tsl], mem.view(TB0 + 4096 * k + tcn * 1024, 512, BF16).keys) for k in range(2)]
                for k in range(2):
                    mm(pa[0:96, :], wqv[:, k, 0, h * 96:(h + 1) * 96], qv[k], k == 0, k == 1)
                for k in range(2):
                    mm(pbb[0:96, :], wqv[:, k, 1, h * 96:(h + 1) * 96], qv[k], k == 0, k == 1)
                qk = mem.view(R10 + 4096 * sl + tcn * 1024, 512, BF16).keys
                vcopy(V(QT[sl].ap[0:64, tsl], qk), pa[0:64, :])
                csv = V(cs_t.ap[:, tsl], cs_t.keys)
                snv = V(sn_t.ap[:, tsl], sn_t.keys)
                vtt(tmpA[64:96, :], pa[64:96, :], csv, ALU.mult)
                vtt(tmpB[64:96, :], pbb[64:96, :], snv, ALU.mult)
                vtt(V(QT[sl].ap[64:96, tsl], qk), tmpA[64:96, :], tmpB[64:96, :], ALU.add)

            def k_piece(tp_):
                S.tag = 'qkv'
                for j in range(2):
                    tcn = tp_ * 2 + j
                    tsl = slice(tcn * 512, (tcn + 1) * 512)
                    pkk = PS(6 + j)
                    mm(pkk[0:64, :], wkv_[:, h * 64:(h + 1) * 64], V(kvlatT.ap[:, tsl], mem.view(TB0 + 8192 + tcn * 1024, 512, BF16).keys), True, True)
                    kk = mem.view(R10 + 8192 + 4096 * sl + tcn * 1024, 512, BF16).keys
                    vcopy(V(KT[sl].ap[0:64, tsl], kk), pkk[0:64, :])
                if tp_ == 1:
                    vcopy(KT[sl][64:96, :], kpeT)

            def v_piece():
                S.tag = 'qkv'
                pv2 = PS(6, 2)
                for kt in range(16):
                    pvv = V(pt[:, 6 + kt // 8, (kt % 8) * 64:(kt % 8 + 1) * 64], [('ps', 6 + kt // 8)])
                    mm(pvv, V(kvlatT.ap[:, kt * 128:(kt + 1) * 128], mem.view(TB0 + 8192 + kt * 256, 128, BF16).keys), wvv[:, h * 64:(h + 1) * 64], True, True)
                voff = 0 if sl == 0 else 64
                vcopy(V(VA[sl].ap[:, :, voff:voff + 64], VA[sl].keys), V(pt[:, 6:8, :].rearrange("p b (a c) -> p (b a) c", c=64), pv2.keys))

            for tcn in range(4):
                pieces.append(lambda tcn=tcn: q_piece(tcn))
            for tp_ in range(2):
                pieces.append(lambda tp_=tp_: k_piece(tp_))
            pieces.append(v_piece)
            return pieces

        _p0 = qkv_pieces(0)
        for p_ in (_p0[0], _p0[4], _p0[5], _p0[6]):
            p_()
        cur_pieces0 = [_p0[1], _p0[2], _p0[3]]
        dump('QT0', QT[0], [128, 2048], BF16)
        dump('KT0', KT[0], [128, 2048], BF16)
        dump('VA0', V(VA[0].ap, VA[0].keys), [128, 16, 128], BF16)
        for h in range(NH):
            sl = h % 2
            steps = [(qc, kp) for qc in range(4) for kp in range(8)]
            obank = [None]
            nxt_pieces = qkv_pieces(h + 1) if h + 1 < NH else []
            ptis = {}

            def emitS(k):
                S.tag = 'attn'
                qc, kp = steps[k]
                s2 = s2r()
                for j in range(2):
                    kt = kp * 2 + j
                    mm(PS(s2 + j), V(KT[sl].ap[0:96, kt * 128:(kt + 1) * 128], mem.view(R10 + 8192 + 4096 * sl + kt * 256, 128, BF16).keys),
                       V(QT[sl].ap[0:96, qc * 512:(qc + 1) * 512], mem.view(R10 + 4096 * sl + qc * 1024, 512, BF16).keys), True, True)
                pti = ptr()
                sin = PS(s2, 2)
                aop(lambda e, pti=pti, sin=sin: e.activation(out=PT[pti].ap, in_=sin.ap, func=AF.Exp, scale=SCALE), [sin], [PT[pti]])
                ptis[k] = pti

            def emitPV(k):
                S.tag = 'attn'
                pqc, pkp = steps[k]
                ppti = ptis[k]
                if pkp == 0:
                    obank[0] = orr()
                ob = PS(obank[0])
                for j in range(2):
                    kt = pkp * 2 + j
                    mm(ob, V(VA[sl].ap[:, kt, :], mem.view(R10 + 16384 + 4096 * sl + kt * 256, 128, BF16).keys), PT[ppti][:, j, :],
                       pkp == 0 and j == 0, pkp == 7 and j == 1)
                if pkp == 7:
                    j2 = h // 2
                    osl = slice(pqc * 512, (pqc + 1) * 512)
                    ok = mem.view(OT0 + 4096 * j2 + pqc * 1024, 512, BF16).keys
                    rd = rden[pqc % 2]
                    if sl == 0:
                        vop(lambda e, ob=ob, rd=rd: e.reciprocal(out=rd.ap[0:64, :], in_=ob.ap[64:128, :]), [ob], [rd])
                        vtt(V(OT[j2].ap[0:64, osl], ok), ob[0:64, :], rd[0:64, :], ALU.mult)
                    else:
                        vop(lambda e, ob=ob, rd=rd: e.reciprocal(out=rd.ap[64:128, :], in_=ob.ap[0:64, :]), [ob], [rd])
                        vtt(V(OT[j2].ap[64:128, osl], ok), ob[64:128, :], rd[64:128, :], ALU.mult)

            nst = len(steps)
            emitS(0)
            emitS(1)
            for k in range(nst):
                if h == 0 and k % 4 == 0 and cur_pieces0:
                    cur_pieces0.pop(0)()
                if k >= 2 and (k - 2) % 4 == 0 and nxt_pieces:
                    nxt_pieces.pop(0)()
                if k + 2 < nst:
                    emitS(k + 2)
                emitPV(k)
        dump('OT', mem.view(OT0, 8192, BF16), [128, 8192], BF16)

    def merge(b, l):
        S.tag = 'merge'
        nctx = rms_stats_begin(NT0)
        wob = RR([4, 5, 6])
        wout = mem.view(R10, 8 * 1024, BF16).r("p (k c) -> p k c", k=8)
        bun = [mem.view(R10 + 16384 + 6144 * i, 3072, BF16) for i in range(2)]
        sg = [mem.view(R10 + 28672 + 2048 * i, 512, F32) for i in range(2)]
        mg = [mem.view(TB0 + 8192 * i, 8 * 512, BF16).r("p (k c) -> p k c", k=8) for i in range(2)]
        wdma(wout, wout_d[l].rearrange("(k p) c -> p k c", p=128), 'wout')
        bc = [0]
        for tcn in range(4):
            ms = tcn % 2
            for fc in range(8):
                sl = bc[0] % 2
                bc[0] += 1
                S.dma('sync', lambda e, sl=sl, fc=fc: e.dma_start(out=bun[sl].ap, in_=mrgb_d[l, fc]), ('bun', sl), reads=[('mrgb', l, fc)], writes=[bun[sl]])
                pga, pgb, pya, pyh = PS(0), PS(1), PS(2), PS(3)
                for k in range(8):
                    mm(pga, bun[sl][:, k * 128:(k + 1) * 128], HTs(k, tcn * 512, 512), k == 0, k == 7)
                for k in range(8):
                    mm(pgb, bun[sl][:, 1024 + k * 128:1024 + (k + 1) * 128], HTs(k, tcn * 512, 512), k == 0, k == 7)
                for j in range(4):
                    mm(pya, bun[sl][:, 2048 + j * 128:2048 + (j + 1) * 128], mem.view(OT0 + 4096 * j + tcn * 1024, 512, BF16), j == 0, j == 3)
                for j in range(4):
                    mm(pyh, bun[sl][:, 2560 + j * 128:2560 + (j + 1) * 128], mem.view(ZY0 + 4096 * j + tcn * 1024, 512, BF16), j == 0, j == 3)
                aop(lambda e: e.activation(out=sg[0].ap, in_=pga.ap, func=AF.Sigmoid), [pga], [sg[0]])
                aop(lambda e: e.activation(out=sg[1].ap, in_=pgb.ap, func=AF.Sigmoid), [pgb], [sg[1]])
                vtt(sg[0], sg[0], pya, ALU.mult)
                vtt(sg[1], sg[1], pyh, ALU.mult)
                dst = V(mg[ms].ap[:, fc, :], mem.view(TB0 + 8192 * ms + fc * 1024, 512, BF16).keys)
                vtt(dst, sg[0], sg[1], ALU.add)
                if fc == 3 and tcn > 0:
                    norm_group_b(tcn - 1)
            for tt in range(4):
                t = tcn * 4 + tt
                for hf in range(2):
                    po = PS(wob())
                    for fc in range(8):
                        mm(po, V(mg[ms].ap[:, fc, tt * 128:(tt + 1) * 128], mem.view(TB0 + 8192 * ms + fc * 1024 + tt * 256, 128, BF16).keys),
                           V(wout.ap[:, fc, hf * 512:(hf + 1) * 512], mem.view(R10 + fc * 2048 + hf * 1024, 512, BF16).keys), fc == 0, fc == 7)
                    xv = V(Xt[t].ap[:, hf * 512:(hf + 1) * 512], mem.view(X0 + t * 4096 + hf * 2048, 512, F32).keys)
                    vtt(xv, xv, po, ALU.add)
            norm_group_a(nctx, tcn, gB2, 'hT')
        norm_group_b(3)

    def ffn(b, l, nxt=None, nb_next=False):
        S.tag = 'ffn'
        fbank = RR([0, 1, 2, 3, 4, 5, 6])
        nctx = rms_stats_begin(NT0) if nxt is not None else None
        act = mem.view(R10, 4 * 2048, BF16).r("p (c t) -> p c t", c=4)
        sgt = [mem.view(R10 + 16384 + 2048 * i, 512, F32) for i in range(2)]
        wsl = [mem.view(ZY0 + 24576 * i, 12288, BF16) for i in range(2)]
        for gi, (g0, gs) in enumerate(FF_GROUPS):
            sl = gi % 2
            nch = gs // 128
            tot = 16 * gs + nch * 1024
            W = wsl[sl]
            S.dma('gpsimd', lambda e, W=W, gi=gi, tot=tot: e.dma_start(out=W.ap[:, 0:tot], in_=ffn_d[l, gi, :, 0:tot]), ('ffw', sl), writes=[W])
            for ch in range(nch):
                for tcn in range(4):
                    pg, pu = PS(fbank()), PS(fbank())
                    for k in range(8):
                        mm(pg, W[:, k * gs + ch * 128:k * gs + (ch + 1) * 128], HTs(k, tcn * 512, 512), k == 0, k == 7)
                    for k in range(8):
                        mm(pu, W[:, 8 * gs + k * gs + ch * 128:8 * gs + k * gs + (ch + 1) * 128], HTs(k, tcn * 512, 512), k == 0, k == 7)
                    st_ = sgt[(ch * 4 + tcn) % 2]
                    aop(lambda e, st_=st_, pg=pg: e.activation(out=st_.ap, in_=pg.ap, func=AF.Silu), [pg], [st_])
                    dst = V(act.ap[:, ch, tcn * 512:(tcn + 1) * 512], mem.view(R10 + ch * 4096 + tcn * 1024, 512, BF16).keys)
                    vtt(dst, st_, pu, ALU.mult)
            for t in range(16):
                for hf in range(2):
                    po = PS(fbank())
                    for ch in range(nch):
                        mm(po, V(act.ap[:, ch, t * 128:(t + 1) * 128], mem.view(R10 + ch * 4096 + t * 256, 128, BF16).keys),
                           W[:, 16 * gs + ch * 1024 + hf * 512:16 * gs + ch * 1024 + (hf + 1) * 512], ch == 0, ch == nch - 1)
                    xv = V(Xt[t].ap[:, hf * 512:(hf + 1) * 512], mem.view(X0 + t * 4096 + hf * 2048, 512, F32).keys)
                    vtt(xv, xv, po, ALU.add)
                if gi == len(FF_GROUPS) - 1 and nxt is not None:
                    if nxt == 'mix':
                        if t % 4 == 1 and t // 4 > 0:
                            norm_group_b(t // 4 - 1)
                        if t % 4 == 3:
                            norm_group_a(nctx, t // 4, gB, 'hT')
                        if t == 15:
                            norm_group_b(3)
                    else:
                        g = t // 4
                        if t % 4 == 1 and g >= 2 and nb_next:
                            norm_group_b(g - 2)
                        if t % 4 == 3:
                            norm_group_a(nctx, g, gB2, 'out', b=b)
                            if nb_next:
                                load_x(b + 1, range(4 * g, 4 * g + 4))
                                if g >= 1:
                                    norm_group_a(nctx, g - 1, gB, 'hT')
            if gi == len(FF_GROUPS) - 1 and nxt == 'final' and nb_next:
                norm_group_b(2)
                norm_group_a(nctx, 3, gB, 'hT')
                norm_group_b(3)

    def final_norm(b):
        S.tag = 'final'
        gB = gB2
        tmp0 = R10
        stg = [mem.view(R10 + 8192 + 4096 * i, 1024, F32) for i in range(4)]
        ctx = rms_stats_begin(tmp0)
        stv, blk, _ = ctx
        rms_stats_act(ctx)
        for t in range(16):
            j = t % 4
            if t % 4 == 0:
                rms_rstd(ctx, t // 4)
            S.op('vector', lambda e, t=t, j=j: e.scalar_tensor_tensor(out=stg[j].ap, in0=Xt[t].ap, scalar=stv.ap[:, 48 + t:49 + t], in1=gB.ap, op0=ALU.mult, op1=ALU.mult),
                 reads=[Xt[t], ('rstd', t // 4), blk, gB], writes=[stg[j]])
            S.dma('sync', lambda e, t=t, j=j: e.dma_start(out=out_d[b, t * 128:(t + 1) * 128, :], in_=stg[j].ap), ('out', j), reads=[stg[j]], writes=[('OUT', b, t)])

    filter_prologue(lambda: load_x(0))
    for l in range(nlayers):
        for fc in range(8):
            S.dma('gpsimd', lambda e, l=l, fc=fc: e.dma_start(out=mrgb_d[l, fc], in_=mrg_d[l, fc]), ('mrgb', l, fc), writes=[('mrgb', l, fc)])
    for j in range(8):
        S.dma('gpsimd', lambda e, j=j: e.dma_start(out=tfwdb_d[j], in_=tfwd_d[j]), ('tfb', j), writes=[('tfb', j)])
    for g in range(4):
        S.dma('gpsimd', lambda e, g=g: e.dma_start(out=tinvb_d[g], in_=tinv_d[g]), ('tib', g), writes=[('tib', g)])

    load_gain(0)
    for b in range(nseq):
        for l in range(nlayers):
            if l == 0 and b == 0:
                norm_to_hT(gB, TB0)
            if b == 0 and l == 0:
                dump('HT', mem.view(HT0, 16384, BF16), [128, 16384], BF16)
            hyena(b, l)
            attention(b, l)
            load_gain(2 + l, gB2)
            merge(b, l)
            if b == 0 and l == 0:
                dump('Xmix', mem.view(X0, 16384, F32), [128, 16384], F32)
            last = (l + 1 == nlayers)
            if not last:
                load_gain(l + 1)
            else:
                load_gain(4, gB2)
                if b + 1 < nseq:
                    load_gain(0)
            ffn(b, l, 'final' if last else 'mix', nb_next=(last and b + 1 < nseq))
            if b == 0 and l == 0:
                dump('Xffn', mem.view(X0, 16384, F32), [128, 16384], F32)
    outs = [('OUT', b, t) for b in range(nseq) for t in range(16)]
    if dbg_out:
        outs.append('DBGOUT')
    S.wait_all('sync', outs)
    S.finalize()
    es.close()
    return nc, S


def _device_inputs(inp, nseq_total=32):
    c = _host_consts()
    w = _prep_weights(inp)
    shared = {}
    for k in ['positions', 'w_in', 'wkpe', 'wq', 'wk', 'wv', 'mrg', 'w_out', 'ffn', 'norm_g', 'cols', 'hy_skip',
              'filt_w1', 'filt_w2', 'filt_w3', 'filt_w4']:
        shared[k] = w[k]
    for k in ['tfwd', 'tinv', 'zembT', 'decay', 'cst', 'cstb', 'altrow']:
        shared[k] = c[k]
    return shared


def kernel(**inputs):
    x = np.ascontiguousarray(np.asarray(inputs['x'], dtype=np.float32))
    shared = _device_inputs(inputs)
    n = 8
    nseq = x.shape[0] // n
    nc, _ = build(nseq=nseq, nlayers=2)
    in_maps = []
    for i in range(n):
        m = dict(shared)
        m['x'] = np.ascontiguousarray(x[i * nseq:(i + 1) * nseq])
        in_maps.append(m)
    res = run_bass_kernel_spmd(nc, in_maps, core_ids=list(range(n)))
    return np.concatenate([np.asarray(r['out']) for r in res.results], axis=0).astype(np.float32)
```
